# Optimizing a Trainium2 kernel written in Bass

```python
import math
import jax, jax.numpy as jnp
from jax import lax
import numpy as np

D_MODEL = 4096
BATCH = 1
SEQ = 16384
DEPTH = 1

N_META = 16
BLOCK_Q = 128
LEAD_PAD = BLOCK_Q - N_META
MIX_WIDTH = D_MODEL
ATTN_WIDTH = MIX_WIDTH // 2
CONV_WIDTH = MIX_WIDTH - ATTN_WIDTH
N_DIFF_HEADS = 8
N_SOFTMAX_MAPS = 2 * N_DIFF_HEADS
DK = ATTN_WIDTH // N_SOFTMAX_MAPS
DV = 2 * DK
IN_WIDTH = 3 * ATTN_WIDTH + 2 * CONV_WIDTH
CONV_KERNEL = 31
D_FF = 4 * D_MODEL
REL_BUCKETS = 32
REL_MAX_EXACT = REL_BUCKETS // 2
REL_MAX_DISTANCE = 128
NORM_EPS = 1e-6
NEG_INF = -1e30

kernel_name = "hymba_diffattn_conformer_conv_hybrid"


def rmsnorm(x, g):
    xf = x.astype(jnp.float32)
    y = xf * lax.rsqrt(jnp.mean(xf * xf, axis=-1, keepdims=True) + NORM_EPS)
    return (y * g.astype(jnp.float32)).astype(x.dtype)


def layernorm(x, g, b):
    xf = x.astype(jnp.float32)
    mu = jnp.mean(xf, axis=-1, keepdims=True)
    var = jnp.mean(jnp.square(xf - mu), axis=-1, keepdims=True)
    y = (xf - mu) * lax.rsqrt(var + NORM_EPS) * g.astype(jnp.float32) + b.astype(jnp.float32)
    return y.astype(x.dtype)


def t5_causal_bucket(dist):
    n = jnp.maximum(dist, 0)
    nf = jnp.maximum(n, 1).astype(jnp.float32)
    large = REL_MAX_EXACT + (jnp.log(nf / REL_MAX_EXACT)
                             / math.log(REL_MAX_DISTANCE / REL_MAX_EXACT)
                             * (REL_BUCKETS - REL_MAX_EXACT)).astype(jnp.int32)
    large = jnp.minimum(large, REL_BUCKETS - 1)
    return jnp.where(n < REL_MAX_EXACT, n, large)


def diff_attention(q, k, v, rel_bias, lam, lambda_init, subln_w):
    B, LP = q.shape[0], q.shape[1]
    nblk = LP // BLOCK_Q
    scale = DK ** -0.5
    qb = q.reshape(B, nblk, BLOCK_Q, N_SOFTMAX_MAPS, DK).transpose(1, 0, 2, 3, 4)
    starts = jnp.arange(nblk, dtype=jnp.int32) * BLOCK_Q
    kpos = jnp.arange(LP, dtype=jnp.int32)
    key_valid = kpos >= LEAD_PAD

    def block(args):
        qblk, start = args
        qpos = start + jnp.arange(BLOCK_Q, dtype=jnp.int32)
        dist = qpos[:, None] - kpos[None, :]
        bias = jnp.transpose(rel_bias[t5_causal_bucket(dist)], (2, 0, 1)).astype(jnp.float32)
        mask = (dist >= 0) & key_valid[None, :]
        s = jnp.einsum('bqhd,bkhd->bhqk', qblk, k).astype(jnp.float32) * scale + bias[None]
        s = jnp.where(mask[None, None], s, NEG_INF)
        p = jax.nn.softmax(s, axis=-1).reshape(B, N_DIFF_HEADS, 2, BLOCK_Q, LP)
        a = p[:, :, 0] - lam * p[:, :, 1]
        return jnp.einsum('bhqk,bkhe->bqhe', a.astype(v.dtype), v)

    out = lax.map(block, (qb, starts))
    out = out.transpose(1, 0, 2, 3, 4).reshape(B, LP, N_DIFF_HEADS, DV)
    out = rmsnorm(out, subln_w) * (1.0 - lambda_init)
    return out.reshape(B, LP, N_DIFF_HEADS * DV)


def conformer_conv(u, valid, conv_w, conv_b, ln_g, ln_b):
    a, gate = jnp.split(u, 2, axis=-1)
    g = a * jax.nn.sigmoid(gate) * valid[None, :, None]
    y = lax.conv_general_dilated(
        g, conv_w[:, None, :].astype(g.dtype), window_strides=(1,),
        padding=[(CONV_KERNEL - 1, 0)],
        dimension_numbers=('NWC', 'WIO', 'NWC'),
        feature_group_count=CONV_WIDTH) + conv_b
    y = layernorm(y, ln_g, ln_b)
    return jax.nn.silu(y)


def setup_inputs(seed: int = 0) -> dict:
    key = jax.random.key(seed)
    ks = jax.random.split(key, 20)
    f32 = jnp.float32
    n = lambda k, shape, s: jax.random.normal(k, shape, f32) * s
    return {
        "x": n(ks[0], (BATCH, SEQ, D_MODEL), 1.0),
        "meta_tokens": n(ks[1], (N_META, D_MODEL), 1.0),
        "rel_bias": n(ks[2], (REL_BUCKETS, N_SOFTMAX_MAPS), 0.5),
        "norm_mix": 1.0 + n(ks[3], (DEPTH, D_MODEL), 0.02),
        "w_in": n(ks[4], (DEPTH, D_MODEL, IN_WIDTH), D_MODEL ** -0.5),
        "lambda_q1": n(ks[5], (DEPTH, DK), 0.1),
        "lambda_k1": n(ks[6], (DEPTH, DK), 0.1),
        "lambda_q2": n(ks[7], (DEPTH, DK), 0.1),
        "lambda_k2": n(ks[8], (DEPTH, DK), 0.1),
        "subln_w": 1.0 + n(ks[9], (DEPTH, DV), 0.02),
        "conv_w": n(ks[10], (DEPTH, CONV_KERNEL, CONV_WIDTH), CONV_KERNEL ** -0.5),
        "conv_b": n(ks[11], (DEPTH, CONV_WIDTH), 0.01),
        "conv_ln_g": 1.0 + n(ks[12], (DEPTH, CONV_WIDTH), 0.02),
        "conv_ln_b": n(ks[13], (DEPTH, CONV_WIDTH), 0.01),
        "merge_scale": 1.0 + n(ks[14], (DEPTH, MIX_WIDTH), 0.02),
        "w_out": n(ks[15], (DEPTH, MIX_WIDTH, D_MODEL), MIX_WIDTH ** -0.5),
        "norm_mlp": 1.0 + n(ks[16], (DEPTH, D_MODEL), 0.02),
        "w_up": n(ks[17], (DEPTH, D_MODEL, D_FF), D_MODEL ** -0.5),
        "w_down": n(ks[18], (DEPTH, D_FF, D_MODEL), D_FF ** -0.5),
        "norm_final": 1.0 + n(ks[19], (D_MODEL,), 0.02),
    }


def reference(x, meta_tokens, rel_bias, norm_mix, w_in, lambda_q1, lambda_k1,
              lambda_q2, lambda_k2, subln_w, conv_w, conv_b, conv_ln_g,
              conv_ln_b, merge_scale, w_out, norm_mlp, w_up, w_down, norm_final):
    B = x.shape[0]
    pad = jnp.zeros((B, LEAD_PAD, D_MODEL), x.dtype)
    meta = jnp.broadcast_to(meta_tokens[None].astype(x.dtype), (B, N_META, D_MODEL))
    h = jnp.concatenate([pad, meta, x], axis=1)
    LP = h.shape[1]
    valid = (jnp.arange(LP) >= LEAD_PAD).astype(h.dtype)

    for l in range(DEPTH):
        lambda_init = 0.8 - 0.6 * math.exp(-0.3 * l)
        xn = rmsnorm(h, norm_mix[l])
        proj = jnp.einsum('bld,de->ble', xn, w_in[l])
        q, k, v, u = jnp.split(proj, [ATTN_WIDTH, 2 * ATTN_WIDTH, 3 * ATTN_WIDTH], axis=-1)
        q = q.reshape(B, LP, N_SOFTMAX_MAPS, DK)
        k = k.reshape(B, LP, N_SOFTMAX_MAPS, DK)
        v = v.reshape(B, LP, N_DIFF_HEADS, DV)
        lam = (jnp.exp(jnp.sum(lambda_q1[l].astype(jnp.float32) * lambda_k1[l].astype(jnp.float32)))
               - jnp.exp(jnp.sum(lambda_q2[l].astype(jnp.float32) * lambda_k2[l].astype(jnp.float32)))
               + lambda_init)
        attn = diff_attention(q, k, v, rel_bias, lam, lambda_init, subln_w[l])
        conv = conformer_conv(u, valid, conv_w[l], conv_b[l], conv_ln_g[l], conv_ln_b[l])
        mixed = jnp.concatenate([attn, conv], axis=-1) * merge_scale[l]
        h = h + jnp.einsum('blm,md->bld', mixed, w_out[l])
        hn = rmsnorm(h, norm_mlp[l])
        hid = jnp.square(jax.nn.relu(jnp.einsum('bld,df->blf', hn, w_up[l])))
        h = h + jnp.einsum('blf,fd->bld', hid, w_down[l])

    out = rmsnorm(h, norm_final)
    return out[:, LEAD_PAD + N_META:]
```

```python
import contextlib
import numpy as np
import concourse.bass as bass
import concourse.mybir as mybir
from concourse.bass_utils import run_bass_kernel_spmd

F32 = mybir.dt.float32
BF16 = mybir.dt.bfloat16
I32 = mybir.dt.int32
CC_INC = 1
AF = mybir.ActivationFunctionType
ALU = mybir.AluOpType

NCORES = 8
N_META = 16
LEAD_PAD = 112
NMAPS = 16
NHEADS = 8
DK = 128
DV = 256
ATTN_W = 2048
CONV_W = 2048
MIX_W = 4096
IN_W = 10240
CONV_K = 31
EPS = 1e-6
LAMBDA_INIT = 0.8 - 0.6 * 1.0


class Cfg:
    def __init__(self, D=4096, DFF=16384, NB=16, TT=512):
        self.D, self.DFF, self.NB, self.TT = D, DFF, NB, TT
        self.KC = D // 128
        self.FC = DFF // 128
        self.T = (NB + 1) * 128
        self.NTILES = NB * 128 // TT
        self.FQ = min(32, self.FC)
        self.NFQ = self.FC // self.FQ
        self.NBO = D // 256
        self.SEQ = NCORES * NB * 128


class Buf:
    __slots__ = ("name", "w", "r")

    def __init__(self, name):
        self.name = name
        self.w = None
        self.r = []


class Sched:
    ENG = ("pe", "act", "dve", "pool", "sp")

    def __init__(self, nc):
        self.nc = nc
        self.sems = {}
        self.cnt = {}
        self.prog = {e: [] for e in self.ENG}
        self.seen = {e: {} for e in self.ENG}
        for e in ("pe", "act", "dve", "pool"):
            self._mksem(e)
        self.nstream = 0

    def _mksem(self, key):
        self.sems[key] = self.nc.alloc_semaphore(name="s_" + key)
        self.cnt[key] = 0

    def stream(self, name):
        key = "q_%s_%d" % (name, self.nstream)
        self.nstream += 1
        self._mksem(key)
        return key

    def _deps(self, reads, writes):
        need = {}

        def add(ev):
            if ev is None:
                return
            k, v = ev
            if need.get(k, 0) < v:
                need[k] = v
        for b in reads:
            add(b.w)
        for b in writes:
            add(b.w)
            for ev in b.r:
                add(ev)
        return need

    def _emit_waits(self, eng, need):
        seen = self.seen[eng]
        for k, v in need.items():
            if seen.get(k, 0) >= v:
                continue
            seen[k] = v
            self.prog[eng].append(("wait", k, v))

    def op(self, eng, fn, reads=(), writes=()):
        need = self._deps(reads, writes)
        self._emit_waits(eng, need)
        self.cnt[eng] += 1
        v = self.cnt[eng]
        self.prog[eng].append(("op", fn, eng, 1))
        ev = (eng, v)
        for b in writes:
            b.w = ev
            b.r = []
        for b in reads:
            if b not in writes:
                b.r.append(ev)
        return ev

    def dma(self, q, stream, fn, reads=(), writes=(), inc=16):
        need = self._deps(reads, writes)
        if self.cnt[stream] > 0:
            if need.get(stream, 0) < self.cnt[stream]:
                need[stream] = self.cnt[stream]
        self._emit_waits(q, need)
        self.cnt[stream] += inc
        v = self.cnt[stream]
        self.prog[q].append(("op", fn, stream, inc))
        ev = (stream, v)
        for b in writes:
            b.w = ev
            b.r = []
        for b in reads:
            if b not in writes:
                b.r.append(ev)
        return ev

    def barrier(self):
        need = {k: v for k, v in self.cnt.items() if v > 0}
        for e in self.ENG:
            self._emit_waits(e, dict(need))

    def emit(self):
        nc = self.nc
        handles = {"pe": "tensor", "act": "scalar", "dve": "vector", "pool": "gpsimd", "sp": "sync"}
        sems = self.sems

        def replay(e, eng):
            for item in self.prog[eng]:
                if item[0] == "wait":
                    e.wait_ge(sems[item[1]], item[2])
                else:
                    _, fn, key, inc = item
                    ins = fn(e)
                    ins.then_inc(sems[key], inc)
        with nc.Block() as block:
            @block.tensor
            def _(e):
                replay(e, "pe")

            @block.scalar
            def _(e):
                replay(e, "act")

            @block.vector
            def _(e):
                replay(e, "dve")

            @block.gpsimd
            def _(e):
                replay(e, "pool")

            @block.sync
            def _(e):
                replay(e, "sp")


class Ring:
    def __init__(self, items):
        self.items = items
        self.i = 0

    def next(self):
        it = self.items[self.i % len(self.items)]
        self.i += 1
        return it


def build(cfg):
    D, DFF, NB, TT, KC, FC, T = cfg.D, cfg.DFF, cfg.NB, cfg.TT, cfg.KC, cfg.FC, cfg.T
    FQ, NFQ, NBO, NTILES = cfg.FQ, cfg.NFQ, cfg.NBO, cfg.NTILES
    NTOK = NCORES * NB * 128
    nc = bass.Bass("TRN2", target_bir_lowering=False)
    S = Sched(nc)
    RG = [list(range(NCORES))]

    def dram_in(name, shape, dt=F32):
        return nc.dram_tensor(name, list(shape), dt, kind="ExternalInput").ap()

    def dram_tmp(name, shape, dt):
        return nc.dram_tensor(name, list(shape), dt).ap()

    hp = dram_in("hp", [T, D])
    w_in_s = dram_in("w_in_s", [5 * 128, KC * 256])
    w_out_s = dram_in("w_out_s", [NBO // 8 * 128, 32 * 256])
    w_up_s = dram_in("w_up_s", [DFF // 256 // 8 * 128, KC * 256])
    w_dn_s = dram_in("w_dn_s", [NFQ * NBO // 8 * 128, FQ * 256])
    g_mix_i = dram_in("g_mix", [128, KC])
    g_mlp_i = dram_in("g_mlp", [128, KC])
    nf_i = dram_in("nf_b", [128, D])
    cw_i = dram_in("cw", [128, 16 * CONV_K])
    cvec_i = dram_in("cvec", [128, 4 * 16])
    avec_i = dram_in("avec", [128, 4])
    lam_i = dram_in("lamv", [128, 4 * 128])
    relb_i = dram_in("relb", [128, 2 * 2 * 128])
    mask_i = dram_in("maskt", [128, 2 * 128])
    b31_i = dram_in("b31", [128, 2])
    NIDX = 32 + NCORES * (NB + 1) + NTILES * 16
    idx_i = dram_in("idxt", [128, NIDX], I32)
    y_out = nc.dram_tensor("y", [NB * 128, D], F32, kind="ExternalOutput").ap()

    w_in_b = dram_tmp("w_in_b", [5 * 128, KC * 256], BF16)
    w_in_g = dram_tmp("w_in_g", [40 * 128, KC * 256], BF16)
    w_out_b = dram_tmp("w_out_b", [NBO // 8 * 128, 32 * 256], BF16)
    w_out_g = dram_tmp("w_out_g", [NBO * 128, 32 * 256], BF16)
    w_up_b = dram_tmp("w_up_b", [DFF // 256 // 8 * 128, KC * 256], BF16)
    w_up_g = dram_tmp("w_up_g", [DFF // 256 * 128, KC * 256], BF16)
    w_dn_b = dram_tmp("w_dn_b", [NFQ * NBO // 8 * 128, FQ * 256], BF16)
    w_dn_g = dram_tmp("w_dn_g", [NFQ * NBO * 128, FQ * 256], BF16)
    qk_loc = dram_tmp("qk_loc", [32 * 128, T], BF16)
    qk_g = dram_tmp("qk_g", [NCORES * 32 * 128, T], BF16)
    v_loc = dram_tmp("v_loc", [T, ATTN_W], BF16)
    v_g = dram_tmp("v_g", [NCORES * T, ATTN_W], BF16)
    gt_loc = dram_tmp("gt_loc", [16 * 128, T], F32)
    at_loc = dram_tmp("at_loc", [NCORES * NTILES * 256, TT], BF16)
    at_g = dram_tmp("at_g", [NCORES * NCORES * NTILES * 256, TT], BF16)
    v_g8 = v_g.rearrange("t (h e) -> (t h) e", e=256)

    B_w_in_g, B_w_out_g, B_w_up_g, B_w_dn_g = Buf("w_in_g"), Buf("w_out_g"), Buf("w_up_g"), Buf("w_dn_g")
    B_qk_loc, B_qk_g, B_v_loc, B_v_g = Buf("qk_loc"), Buf("qk_g"), Buf("v_loc"), Buf("v_g")
    B_gt_loc, B_at_loc, B_at_g = Buf("gt_loc"), Buf("at_loc"), Buf("at_g")

    top = contextlib.ExitStack()

    def sb(stack, name, shape, dt):
        return stack.enter_context(nc.sbuf_tensor(name, list(shape), dt))

    def ps(stack, name, shape, dt):
        return stack.enter_context(nc.psum_tensor(name, list(shape), dt))

    ident = sb(top, "ident", [128, 128], BF16)
    ones_f = sb(top, "ones_f", [128, 128], F32)
    g_mix = sb(top, "g_mix_t", [128, KC], F32)
    g_mlp = sb(top, "g_mlp_t", [128, KC], F32)
    idx_t = sb(top, "idx_t", [128, NIDX], I32)
    cw = sb(top, "cw_t", [128, 16, CONV_K], F32)
    cvec = sb(top, "cvec_t", [128, 4, 16], F32)
    avec = sb(top, "avec_t", [128, 4], F32)
    wsc = sb(top, "wsc", [128, 2], F32)
    lamv = sb(top, "lamv_t", [128, 4, 128], F32)
    lamtmp = sb(top, "lamtmp", [128, 128], F32)
    lamd = sb(top, "lamd", [128, 4], F32)
    neg_lam = sb(top, "neg_lam", [128, 1], F32)
    relb = sb(top, "relb_t", [128, 2, 2, 128], F32)
    maskt = sb(top, "mask_t", [128, 2, 128], F32)
    b31 = sb(top, "b31_t", [128, 2], F32)
    Et = sb(top, "Et", [128, 2, 2, 128], F32)
    eps_t = sb(top, "eps_t", [128, 1], F32)
    B_const = Buf("const")
    q_const = S.stream("const")

    eye_i = dram_in("eye", [128, 128], BF16)

    def ld_const(dst, src):
        S.dma("sp", q_const, lambda e: e.dma_start(out=dst, in_=src), writes=[B_const])
    ld_const(ident[:], eye_i)
    ld_const(g_mix[:], g_mix_i)
    ld_const(g_mlp[:], g_mlp_i)
    ld_const(cw[:], cw_i.rearrange("p (c k) -> p c k", k=CONV_K))
    ld_const(cvec[:], cvec_i.rearrange("p (a c) -> p a c", c=16))
    ld_const(avec[:], avec_i)
    ld_const(lamv[:], lam_i.rearrange("p (a c) -> p a c", c=128))
    ld_const(relb[:], relb_i.rearrange("p (m s q) -> p m s q", m=2, s=2))
    ld_const(maskt[:], mask_i.rearrange("p (s q) -> p s q", s=2))
    ld_const(b31[:], b31_i)
    ld_const(idx_t[:], idx_i)

    def cop(eng, fn):
        S.op(eng, fn, reads=[B_const], writes=[B_const])
    cop("dve", lambda e: e.memset(ones_f[:], 1.0))
    cop("dve", lambda e: e.memset(eps_t[:], EPS))
    cop("dve", lambda e: e.scalar_tensor_tensor(out=wsc[:], in0=avec[:, 0:2], scalar=float(1.0 - LAMBDA_INIT),
                                                in1=avec[:, 2:4], op0=ALU.mult, op1=ALU.mult))
    for i in range(2):
        cop("dve", lambda e, i=i: e.scalar_tensor_tensor(out=lamtmp[:], in0=lamv[:, 2 * i, :], scalar=1.0,
                                                         in1=lamv[:, 2 * i + 1, :], op0=ALU.mult, op1=ALU.mult,
                                                         accum_out=lamd[:, i:i + 1]))
    cop("act", lambda e: e.activation(out=lamd[:, 2:4], in_=lamd[:, 0:2], func=AF.Exp))
    cop("dve", lambda e: e.tensor_tensor(out=neg_lam[:], in0=lamd[:, 3:4], in1=lamd[:, 2:3], op=ALU.subtract))
    cop("dve", lambda e: e.tensor_scalar(out=neg_lam[:], in0=neg_lam[:], scalar1=float(-LAMBDA_INIT), scalar2=None,
                                         op0=ALU.add))
    cop("dve", lambda e: e.tensor_scalar(out=b31[:], in0=b31[:], scalar1=-1.0, scalar2=None, op0=ALU.mult))
    for m in range(2):
        cop("act", lambda e, m=m: e.activation(out=Et[:, m], in_=relb[:, m], func=AF.Exp, bias=b31[:, m:m + 1], scale=1.0))
        cop("dve", lambda e, m=m: e.tensor_tensor(out=Et[:, m], in0=Et[:, m], in1=maskt[:], op=ALU.mult))

    q_wcast = S.stream("wcast")
    q_cc = S.stream("cc")
    for (src, bnc, gat, B_g) in ((w_in_s, w_in_b, w_in_g, B_w_in_g), (w_out_s, w_out_b, w_out_g, B_w_out_g),
                                 (w_up_s, w_up_b, w_up_g, B_w_up_g), (w_dn_s, w_dn_b, w_dn_g, B_w_dn_g)):
        Bb = Buf("bounce")
        rows = src.shape[0]
        step = 128
        for r0 in range(0, rows, step):
            S.dma("pool", q_wcast, lambda e, src=src, bnc=bnc, r0=r0: e.dma_start(out=bnc[r0:r0 + step, :], in_=src[r0:r0 + step, :]),
                  writes=[Bb])
        S.dma("pool", q_cc, lambda e, bnc=bnc, gat=gat: e.collective_compute(
            "AllGather", ALU.bypass, replica_groups=RG, ins=[bnc], outs=[gat]), reads=[Bb], writes=[B_g], inc=CC_INC)

    def norm_transpose(stk_bufs, src, B_src, gB, dstT, B_dst, col0):
        ssq, B_ss, xn, B_xn, pst_ring = stk_bufs
        S.op("dve", lambda e: e.scalar_tensor_tensor(out=xn[:], in0=src, scalar=1.0, in1=src, op0=ALU.mult, op1=ALU.mult,
                                                     accum_out=ssq[:, 0:1]), reads=[B_src], writes=[B_xn, B_ss])
        S.op("act", lambda e: e.activation(out=ssq[:, 1:2], in_=ssq[:, 0:1], func=AF.Sqrt, bias=eps_t[:], scale=1.0 / D),
             reads=[B_ss, B_const], writes=[B_ss])
        S.op("dve", lambda e: e.reciprocal(out=ssq[:, 2:3], in_=ssq[:, 1:2]), reads=[B_ss], writes=[B_ss])
        S.op("act", lambda e: e.activation(out=xn[:], in_=src, func=AF.Copy, scale=ssq[:, 2:3]), reads=[B_src, B_ss], writes=[B_xn])
        for k4 in range(0, KC, 4):
            pt, B_pt = pst_ring.next()

            def tr(e, k4=k4, pt=pt):
                ins = None
                for i in range(4):
                    ins = e.transpose(out=pt[:, i * 128:(i + 1) * 128], in_=xn[:, (k4 + i) * 128:(k4 + i + 1) * 128], identity=ident[:])
                return ins
            S.op("pe", tr, reads=[B_xn, B_const], writes=[B_pt])
            def ev(e, k4=k4, pt=pt):
                ins = None
                for i in range(4):
                    ins = e.tensor_scalar(out=dstT[:, k4 + i, col0:col0 + 128], in0=pt[:, i * 128:(i + 1) * 128],
                                          scalar1=gB[:, k4 + i:k4 + i + 1], scalar2=None, op0=ALU.mult)
                return ins
            S.op("dve", ev, reads=[B_pt, B_const], writes=[B_dst])

    p1 = contextlib.ExitStack()
    xnT = sb(p1, "xnT", [128, KC, T], BF16)
    B_xnT = Buf("xnT")
    xts = None
    p1a = contextlib.ExitStack()
    xn = sb(p1a, "xn", [128, D], BF16)
    xts = [(sb(p1a, "xt%d" % i, [128, D], F32), Buf("xt%d" % i), S.stream("xt")) for i in range(2)]
    ssq = sb(p1a, "ssq", [128, 4], F32)
    psts = Ring([(ps(p1a, "pst%d" % i, [128, 1024], BF16), Buf("pst%d" % i)) for i in range(2)])
    nt_bufs = (ssq, Buf("ssq"), xn, Buf("xn"), psts)
    for b in range(NB + 1):
        xt, B_xt, q_xt = xts[b % 2]
        S.dma("sp", q_xt, lambda e, xt=xt, b=b: e.dma_start(out=xt[:], in_=hp[b * 128:(b + 1) * 128, :]), writes=[B_xt])
        norm_transpose(nt_bufs, xt[:], B_xt, g_mix, xnT, B_xnT, b * 128)
    S.barrier()
    p1a.close()

    wbufs = Ring([(sb(p1, "wb%d" % i, [128, KC, 256], BF16), Buf("wb%d" % i), S.stream("wb")) for i in range(2)])
    pmm = Ring([(ps(p1, "pmm%d" % i, [128, 512], F32), Buf("pmm%d" % i)) for i in range(6)])
    qk_st = Ring([(sb(p1, "qkst%d" % i, [128, 512], BF16), Buf("qkst%d" % i), S.stream("qkst")) for i in range(3)])
    v_st = Ring([(sb(p1, "vst%d" % i, [128, 256], BF16), Buf("vst%d" % i), S.stream("vst")) for i in range(3)])
    g_st = Ring([(sb(p1, "gst%d" % i, [128, 512], F32), Buf("gst%d" % i), S.stream("gst")) for i in range(3)])
    sig = Ring([(sb(p1, "sig%d" % i, [128, 512], F32), Buf("sig%d" % i)) for i in range(2)])
    tchunks = [(t0, min(512, T - t0)) for t0 in range(0, T, 512)]

    def load_w(cb):
        wb, B_wb, q_wb = wbufs.next()
        S.dma("sp", q_wb, lambda e, wb=wb, cb=cb: e.dma_start(
            out=wb[:].rearrange("p k c -> p (k c)"), in_=w_in_g[cb * 128:(cb + 1) * 128, :]), reads=[B_w_in_g], writes=[B_wb])
        return wb, B_wb

    def fm_matmul(wb, B_wb, half, t0, n):
        pm, B_pm = pmm.next()

        def mm(e):
            ins = None
            for kc in range(KC):
                ins = e.matmul(pm[:, 0:n], lhsT=wb[:, kc, half * 128:(half + 1) * 128], rhs=xnT[:, kc, t0:t0 + n],
                               start=(kc == 0), stop=(kc == KC - 1))
            return ins
        S.op("pe", mm, reads=[B_wb, B_xnT], writes=[B_pm])
        return pm, B_pm

    evac_flip = [0]

    def evac_copy(dst, B_dst, src, B_s):
        evac_flip[0] ^= 1
        if evac_flip[0]:
            S.op("act", lambda e: e.activation(out=dst, in_=src, func=AF.Copy), reads=[B_s], writes=[B_dst])
        else:
            S.op("dve", lambda e: e.tensor_copy(out=dst, in_=src), reads=[B_s], writes=[B_dst])

    for cb in range(16):
        wb, B_wb = load_w(cb)
        for half in range(2):
            r0 = cb * 256 + half * 128
            for (t0, n) in tchunks:
                st, B_st, q_st = qk_st.next()
                pm, B_pm = fm_matmul(wb, B_wb, half, t0, n)
                evac_copy(st[:, 0:n], B_st, pm[:, 0:n], B_pm)
                S.dma("sp", q_st, lambda e, st=st, r0=r0, t0=t0, n=n: e.dma_start(out=qk_loc[r0:r0 + 128, t0:t0 + n], in_=st[:, 0:n]),
                      reads=[B_st], writes=[B_qk_loc])
    for hh in range(8):
        wb, B_wb = load_w(16 + hh)
        for tb in range(NB + 1):
            pm, B_pm = pmm.next()

            def mmv(e, pm=pm, wb=wb, tb=tb):
                ins = None
                for kc in range(KC):
                    ins = e.matmul(pm[:, 0:256], lhsT=xnT[:, kc, tb * 128:(tb + 1) * 128], rhs=wb[:, kc, :],
                                   start=(kc == 0), stop=(kc == KC - 1))
                return ins
            S.op("pe", mmv, reads=[B_wb, B_xnT], writes=[B_pm])
            st, B_st, q_st = v_st.next()
            evac_copy(st[:], B_st, pm[:, 0:256], B_pm)
            S.dma("sp", q_st, lambda e, st=st, tb=tb, hh=hh: e.dma_start(
                out=v_loc[tb * 128:(tb + 1) * 128, hh * 256:(hh + 1) * 256], in_=st[:]), reads=[B_st], writes=[B_v_loc])
    S.dma("pool", q_cc, lambda e: e.collective_compute("AllGather", ALU.bypass, replica_groups=RG, ins=[qk_loc], outs=[qk_g]),
          reads=[B_qk_loc], writes=[B_qk_g], inc=CC_INC)
    S.dma("pool", q_cc, lambda e: e.collective_compute("AllGather", ALU.bypass, replica_groups=RG, ins=[v_loc], outs=[v_g]),
          reads=[B_v_loc], writes=[B_v_g], inc=CC_INC)
    for j in range(8):
        wa, B_wa = load_w(24 + j)
        wg, B_wg = load_w(32 + j)
        for half in range(2):
            r0 = (j * 2 + half) * 128
            for (t0, n) in tchunks:
                st, B_st, q_st = g_st.next()
                pg, B_pg = fm_matmul(wg, B_wg, half, t0, n)
                pa, B_pa = fm_matmul(wa, B_wa, half, t0, n)
                sg, B_sg = sig.next()
                S.op("act", lambda e, sg=sg, pg=pg, n=n: e.activation(out=sg[:, 0:n], in_=pg[:, 0:n], func=AF.Sigmoid),
                     reads=[B_pg], writes=[B_sg])
                S.op("dve", lambda e, st=st, pa=pa, sg=sg, n=n: e.tensor_tensor(
                    out=st[:, 0:n], in0=pa[:, 0:n], in1=sg[:, 0:n], op=ALU.mult), reads=[B_pa, B_sg], writes=[B_st])
                S.dma("sp", q_st, lambda e, st=st, r0=r0, t0=t0, n=n: e.dma_start(out=gt_loc[r0:r0 + 128, t0:t0 + n], in_=st[:, 0:n]),
                      reads=[B_st], writes=[B_gt_loc])
    S.barrier()
    p1.close()

    p2 = contextlib.ExitStack()
    KT = sb(p2, "KT", [128, 2, NCORES, T], BF16)
    Vr = sb(p2, "Vr", [128, NCORES, NB + 1, 257], BF16)
    B_KT, B_Vr = Buf("KT"), Buf("Vr")
    q_kv = [S.stream("kv") for _ in range(4)]
    S.op("pool", lambda e: e.memset(Vr[:, :, :, 256:257], 1.0), writes=[B_Vr])
    def gather(q, out, src, col, reads, writes):
        S.dma("pool", q, lambda e: e.indirect_dma_start(out=out, out_offset=None, in_=src,
                                                        in_offset=bass.IndirectOffsetOnAxis(ap=idx_t[:, col:col + 1], axis=0)),
              reads=list(reads) + [B_const], writes=writes)
    for r in range(NCORES):
        for m in range(2):
            gather(q_kv[(2 * r + m) % 4], KT[:, m, r, :], qk_g[:, :], (r * 2 + 1) * 2 + m, [B_qk_g], [B_KT])
        for tb in range(NB + 1):
            gather(q_kv[tb % 4], Vr[:, r, tb, 0:256], v_g8[:, :], 32 + r * (NB + 1) + tb, [B_v_g], [B_Vr])
    S.op("pool", lambda e: e.memset(Vr[0:LEAD_PAD, 0, 0, :], 0.0), writes=[B_Vr])

    qrows = [(sb(p2, "qrow%d" % i, [128, 2, T], BF16), Buf("qrow%d" % i), S.stream("qrow")) for i in range(2)]
    cur_rq = [-1]
    pss = Ring([(ps(p2, "pss%d" % i, [128, 512], F32), Buf("pss%d" % i)) for i in range(3)])
    pso = [(ps(p2, "pso%d" % i, [128, 512], F32), Buf("pso%d" % i)) for i in range(4)]
    pstr = (ps(p2, "pstr", [128, 1024], BF16), Buf("pstr"))
    pts = Ring([(sb(p2, "pt%d" % i, [128, 512], BF16), Buf("pt%d" % i)) for i in range(3)])
    on = [(sb(p2, "on%d" % m, [128, 4, 256], F32), Buf("on%d" % m)) for m in range(2)]
    rs = sb(p2, "rs", [128, 8], F32)
    B_rs = Buf("rs")
    df = sb(p2, "df", [128, 4, 256], F32)
    B_df = Buf("df")
    dj = sb(p2, "dj", [128, 256], F32)
    B_dj = Buf("dj")
    sq = sb(p2, "sq", [128, 12], F32)
    B_sq = Buf("sq")
    dn = sb(p2, "dn", [128, 4, 256], BF16)
    B_dn = Buf("dn")
    ats = Ring([(sb(p2, "ats%d" % i, [128, 2, 512], BF16), Buf("ats%d" % i), S.stream("ats")) for i in range(2)])
    scale = float(DK ** -0.5)

    def blk_col(J):
        if J == 0:
            return 0, 0
        r, tb = (J - 1) // NB, (J - 1) % NB + 1
        return r, tb

    NG = NCORES * NB // 4
    for gi in range(NG):
        Jq0 = 1 + 4 * gi
        rq, tbq = blk_col(Jq0)
        qt, B_qt, q_qt = qrows[rq % 2]
        if cur_rq[0] != rq:
            cur_rq[0] = rq
            for m in range(2):
                gather(q_qt, qt[:, m, :], qk_g[:, :], (rq * 2 + 0) * 2 + m, [B_qk_g], [B_qt])
        qoff = tbq * 128
        for m in range(2):
            for J in range(0, Jq0 + 4):
                qlo = max(0, J - Jq0)
                c0 = qlo * 128
                rk, tbk = blk_col(J)
                psS, B_psS = pss.next()
                S.op("pe", lambda e, psS=psS, m=m, rk=rk, tbk=tbk, qt=qt, c0=c0, qoff=qoff: e.matmul(
                    psS[:, c0:512], lhsT=KT[:, m, rk, tbk * 128:(tbk + 1) * 128], rhs=qt[:, m, qoff + c0:qoff + 512], start=True, stop=True),
                    reads=[B_KT, B_qt], writes=[B_psS])
                pt, B_pt = pts.next()
                S.op("act", lambda e, pt=pt, psS=psS, c0=c0: e.activation(out=pt[:, c0:512], in_=psS[:, c0:512], func=AF.Exp, scale=scale),
                     reads=[B_psS], writes=[B_pt])
                for qb in range(qlo, 4):
                    Jq = Jq0 + qb
                    if J == Jq or J == Jq - 1:
                        s = 1 if J == Jq else 0
                        S.op("dve", lambda e, pt=pt, qb=qb, m=m, s=s: e.tensor_tensor(
                            out=pt[:, qb * 128:(qb + 1) * 128], in0=pt[:, qb * 128:(qb + 1) * 128], in1=Et[:, m, s, :], op=ALU.mult),
                            reads=[B_pt, B_const], writes=[B_pt])

                def pv(e, pt=pt, qlo=qlo, J=J, rk=rk, tbk=tbk):
                    ins = None
                    for qb in range(qlo, 4):
                        ins = e.matmul(pso[qb][0][:, 0:257], lhsT=pt[:, qb * 128:(qb + 1) * 128], rhs=Vr[:, rk, tbk, :],
                                       start=(J == 0), stop=(J == Jq0 + qb))
                    return ins
                S.op("pe", pv, reads=[B_pt, B_Vr], writes=[pso[qb][1] for qb in range(qlo, 4)])
            for qb in range(4):
                po, B_po = pso[qb]
                S.op("dve", lambda e, po=po, qb=qb, m=m: e.reciprocal(out=rs[:, m * 4 + qb:m * 4 + qb + 1], in_=po[:, 256:257]),
                     reads=[B_po], writes=[B_rs])
                S.op("act", lambda e, po=po, qb=qb, m=m: e.activation(out=on[m][0][:, qb, :], in_=po[:, 0:256], func=AF.Copy,
                                                                      scale=rs[:, m * 4 + qb:m * 4 + qb + 1]),
                     reads=[B_po, B_rs], writes=[on[m][1]])
        S.op("dve", lambda e: e.scalar_tensor_tensor(out=df[:], in0=on[1][0][:], scalar=neg_lam[:, 0:1], in1=on[0][0][:],
                                                     op0=ALU.mult, op1=ALU.add), reads=[on[0][1], on[1][1], B_const], writes=[B_df])
        for qb in range(4):
            S.op("dve", lambda e, qb=qb: e.scalar_tensor_tensor(out=dj[:], in0=df[:, qb, :], scalar=1.0, in1=df[:, qb, :], op0=ALU.mult,
                                                                op1=ALU.mult, accum_out=sq[:, qb:qb + 1]), reads=[B_df], writes=[B_dj, B_sq])
        S.op("act", lambda e: e.activation(out=sq[:, 4:8], in_=sq[:, 0:4], func=AF.Sqrt, bias=eps_t[:], scale=1.0 / DV),
             reads=[B_sq, B_const], writes=[B_sq])
        S.op("dve", lambda e: e.reciprocal(out=sq[:, 8:12], in_=sq[:, 4:8]), reads=[B_sq], writes=[B_sq])
        for qb in range(4):
            S.op("act", lambda e, qb=qb: e.activation(out=dn[:, qb, :], in_=df[:, qb, :], func=AF.Copy, scale=sq[:, 8 + qb:9 + qb]),
                 reads=[B_df, B_sq], writes=[B_dn])
        at, B_at, q_at = ats.next()
        ptr, B_ptr = pstr
        for ec in range(2):
            def tr2(e, ec=ec):
                ins = None
                for qb in range(4):
                    ins = e.transpose(out=ptr[:, qb * 128:(qb + 1) * 128], in_=dn[:, qb, ec * 128:(ec + 1) * 128], identity=ident[:])
                return ins
            S.op("pe", tr2, reads=[B_dn, B_const], writes=[B_ptr])
            S.op("dve", lambda e, ec=ec, at=at: e.tensor_scalar(out=at[:, ec, :], in0=ptr[:, 0:512], scalar1=wsc[:, ec:ec + 1],
                                                                scalar2=None, op0=ALU.mult), reads=[B_ptr, B_const], writes=[B_at])
        tsl, off = (gi * 512) // (NB * 128), (gi * 512) % (NB * 128)
        for sub in range(512 // TT):
            ti_ = off // TT + sub
            base = (tsl * NTILES + ti_) * 256
            S.dma("sp", q_at, lambda e, at=at, base=base, sub=sub: e.dma_start(
                out=at_loc[base:base + 256, :].rearrange("(c p) t -> p c t", p=128), in_=at[:, :, sub * TT:(sub + 1) * TT]),
                reads=[B_at], writes=[B_at_loc])
    S.dma("pool", q_cc, lambda e: e.collective_compute("AllGather", ALU.bypass, replica_groups=RG, ins=[at_loc], outs=[at_g]),
          reads=[B_at_loc], writes=[B_at_g], inc=CC_INC)
    S.barrier()
    p2.close()

    p3 = contextlib.ExitStack()
    NTB = TT // 128
    HC = max(NTB * D, 24 * TT)
    hbuf = sb(p3, "hbuf", [128, HC], F32)
    B_h = Buf("h")
    NCH = max(KC, 32, FQ)
    Abuf = sb(p3, "Abuf", [128, NCH, TT], BF16)
    Bbuf = sb(p3, "Bbuf", [128, NCH, TT], BF16)
    B_A, B_B = Buf("A"), Buf("B")
    xn = sb(p3, "xn3", [128, D], BF16)
    nfB = sb(p3, "nfB", [128, D], F32)
    ld_const(nfB[:], nf_i)
    ssq = sb(p3, "ssq3", [128, 4], F32)
    psts = Ring([(ps(p3, "pst3_%d" % i, [128, 1024], BF16), Buf("pst3_%d" % i)) for i in range(2)])
    nt_bufs = (ssq, Buf("ssq3"), xn, Buf("xn3"), psts)
    pmm = Ring([(ps(p3, "pm3_%d" % i, [128, 512], F32), Buf("pm3_%d" % i)) for i in range(4)])
    pln = [(ps(p3, "pln%d" % i, [128, 512], F32), Buf("pln%d" % i)) for i in range(2)]
    wbufs = Ring([(sb(p3, "w3_%d" % i, [128, NCH * 256], BF16), Buf("w3_%d" % i), S.stream("w3")) for i in range(2)])
    gts = Ring([(sb(p3, "gt%d" % i, [128, TT + 32], F32), Buf("gt%d" % i), S.stream("gt")) for i in range(2)])
    rl = Ring([(sb(p3, "rl%d" % i, [128, TT], F32), Buf("rl%d" % i)) for i in range(2)])
    q_x = S.stream("x3")
    q_at3 = S.stream("at3")
    q_y = S.stream("yout")
    yall = hbuf[:, 0:16 * TT].rearrange("p (c t) -> p c t", t=TT)
    tmpv = hbuf[:, 16 * TT:24 * TT].rearrange("p (c t) -> p c t", t=TT)
    hv = hbuf[:, 0:NTB * D].rearrange("p (b d) -> p b d", d=D)

    for ti in range(NTILES):
        tl0 = 128 + ti * TT
        mm_sum = []
        for cc in range(16):
            gt, B_gt, q_gt = gts.next()
            S.dma("sp", q_gt, lambda e, gt=gt, cc=cc, tl0=tl0: e.dma_start(
                out=gt[:], in_=gt_loc[cc * 128:(cc + 1) * 128, tl0 - 32:tl0 + TT]), reads=[B_gt_loc], writes=[B_gt])

            def conv(e, gt=gt, cc=cc):
                ins = e.tensor_scalar(out=yall[:, cc, :], in0=gt[:, 2:2 + TT], scalar1=cw[:, cc, 0:1], scalar2=cvec[:, 0, cc:cc + 1],
                                      op0=ALU.mult, op1=ALU.add)
                return ins
            S.op("dve", conv, reads=[B_gt, B_const], writes=[B_h])
            for j in range(1, CONV_K):
                S.op("dve", lambda e, gt=gt, cc=cc, j=j: e.scalar_tensor_tensor(
                    out=yall[:, cc, :], in0=gt[:, 2 + j:2 + j + TT], scalar=cw[:, cc, j:j + 1], in1=yall[:, cc, :],
                    op0=ALU.mult, op1=ALU.add), reads=[B_gt, B_const, B_h], writes=[B_h])
        pmu, B_pmu = pln[0]
        pq, B_pq = pln[1]
        B_tmp = B_h
        for cc in range(16):
            S.op("pe", lambda e, cc=cc: e.matmul(pmu[:, 0:TT], lhsT=ones_f[:], rhs=yall[:, cc, :], start=(cc == 0), stop=(cc == 15)),
                 reads=[B_h, B_const], writes=[B_pmu])
        for cc in range(16):
            S.op("act", lambda e, cc=cc: e.activation(out=tmpv[:, cc % 2, :], in_=yall[:, cc, :], func=AF.Square), reads=[B_h], writes=[B_h])
            S.op("pe", lambda e, cc=cc: e.matmul(pq[:, 0:TT], lhsT=ones_f[:], rhs=tmpv[:, cc % 2, :], start=(cc == 0), stop=(cc == 15)),
                 reads=[B_h, B_const], writes=[B_pq])
        inv = 1.0 / CONV_W
        S.op("act", lambda e: e.activation(out=tmpv[:, 2, :], in_=pmu[:, 0:TT], func=AF.Copy, scale=inv), reads=[B_pmu], writes=[B_h])
        S.op("dve", lambda e: e.tensor_tensor(out=tmpv[:, 3, :], in0=tmpv[:, 2, :], in1=tmpv[:, 2, :], op=ALU.mult), reads=[B_h], writes=[B_h])
        S.op("dve", lambda e: e.scalar_tensor_tensor(out=tmpv[:, 3, :], in0=pq[:, 0:TT], scalar=inv, in1=tmpv[:, 3, :], op0=ALU.mult,
                                                     op1=ALU.subtract), reads=[B_pq, B_h], writes=[B_h])
        S.op("act", lambda e: e.activation(out=tmpv[:, 3, :], in_=tmpv[:, 3, :], func=AF.Sqrt, bias=eps_t[:], scale=1.0),
             reads=[B_h, B_const], writes=[B_h])
        S.op("dve", lambda e: e.reciprocal(out=tmpv[:, 4, :], in_=tmpv[:, 3, :]), reads=[B_h], writes=[B_h])
        S.op("dve", lambda e: e.scalar_tensor_tensor(out=tmpv[:, 5, :], in0=tmpv[:, 2, :], scalar=-1.0, in1=tmpv[:, 4, :], op0=ALU.mult,
                                                     op1=ALU.mult), reads=[B_h], writes=[B_h])
        for cc in range(16):
            z = tmpv[:, 6 + cc % 2, :]
            S.op("dve", lambda e, cc=cc, z=z: e.tensor_tensor(out=z, in0=yall[:, cc, :], in1=tmpv[:, 4, :], op=ALU.mult), reads=[B_h], writes=[B_h])
            S.op("pool", lambda e, z=z: e.tensor_tensor(out=z, in0=z, in1=tmpv[:, 5, :], op=ALU.add), reads=[B_h], writes=[B_h])
            S.op("act", lambda e, cc=cc, z=z: e.activation(out=z, in_=z, func=AF.Silu, scale=cvec[:, 1, cc:cc + 1], bias=cvec[:, 2, cc:cc + 1]),
                 reads=[B_h, B_const], writes=[B_h])
            S.op("pool", lambda e, cc=cc, z=z: e.tensor_scalar(out=Abuf[:, 16 + cc, :], in0=z, scalar1=cvec[:, 3, cc:cc + 1], scalar2=None,
                                                               op0=ALU.mult), reads=[B_h, B_const], writes=[B_A])
        gtok0 = ti * TT
        for hc in range(16):
            gather(q_at3, Abuf[:, hc, :], at_g[:, :], 32 + NCORES * (NB + 1) + ti * 16 + hc, [B_at_g], [B_A])
        S.dma("sp", q_x, lambda e, tl0=tl0: e.dma_start(out=hv[:, :, :], in_=hp[tl0:tl0 + TT, :].rearrange("(b p) d -> p b d", p=128)),
              writes=[B_h])
        for nb in range(NBO):
            wb, B_wb, q_wb = wbufs.next()
            S.dma("sp", q_wb, lambda e, wb=wb, nb=nb: e.dma_start(out=wb[:, 0:32 * 256], in_=w_out_g[nb * 128:(nb + 1) * 128, :]),
                  reads=[B_w_out_g], writes=[B_wb])
            for tb in range(NTB):
                pm, B_pm = pmm.next()

                def mmo(e, pm=pm, wb=wb, tb=tb):
                    ins = None
                    for ck in range(32):
                        ins = e.matmul(pm[:, 0:256], lhsT=Abuf[:, ck, tb * 128:(tb + 1) * 128], rhs=wb[:, ck * 256:(ck + 1) * 256],
                                       start=(ck == 0), stop=(ck == 31))
                    return ins
                S.op("pe", mmo, reads=[B_A, B_wb], writes=[B_pm])
                S.op("dve", lambda e, pm=pm, tb=tb, nb=nb: e.tensor_tensor(
                    out=hv[:, tb, nb * 256:(nb + 1) * 256], in0=pm[:, 0:256], in1=hv[:, tb, nb * 256:(nb + 1) * 256], op=ALU.add),
                    reads=[B_pm, B_h], writes=[B_h])
        for tb in range(NTB):
            norm_transpose(nt_bufs, hv[:, tb, :], B_h, g_mlp, Bbuf, B_B, tb * 128)
        for fq in range(NFQ):
            for fb in range(FQ // 2):
                wb, B_wb, q_wb = wbufs.next()
                fbg = fq * (FQ // 2) + fb
                S.dma("sp", q_wb, lambda e, wb=wb, fbg=fbg: e.dma_start(out=wb[:, 0:KC * 256], in_=w_up_g[fbg * 128:(fbg + 1) * 128, :]),
                      reads=[B_w_up_g], writes=[B_wb])
                for half in range(2):
                    pm, B_pm = pmm.next()

                    def mmu(e, pm=pm, wb=wb, half=half):
                        ins = None
                        for kc in range(KC):
                            ins = e.matmul(pm[:, 0:TT], lhsT=wb[:, kc * 256 + half * 128:kc * 256 + (half + 1) * 128], rhs=Bbuf[:, kc, :],
                                           start=(kc == 0), stop=(kc == KC - 1))
                        return ins
                    S.op("pe", mmu, reads=[B_B, B_wb], writes=[B_pm])
                    r_, B_r = rl.next()
                    S.op("act", lambda e, r_=r_, pm=pm: e.activation(out=r_[:], in_=pm[:, 0:TT], func=AF.Relu), reads=[B_pm], writes=[B_r])
                    fcl = fb * 2 + half
                    S.op("pool" if (fcl % 2) else "dve", lambda e, r_=r_, fcl=fcl: e.tensor_tensor(out=Abuf[:, fcl, :], in0=r_[:], in1=r_[:], op=ALU.mult),
                         reads=[B_r], writes=[B_A])
            for nb in range(NBO):
                wb, B_wb, q_wb = wbufs.next()
                S.dma("sp", q_wb, lambda e, wb=wb, fq=fq, nb=nb: e.dma_start(
                    out=wb[:, 0:FQ * 256], in_=w_dn_g[(fq * NBO + nb) * 128:(fq * NBO + nb + 1) * 128, :]), reads=[B_w_dn_g], writes=[B_wb])
                for tb in range(NTB):
                    pm, B_pm = pmm.next()

                    def mmd(e, pm=pm, wb=wb, tb=tb):
                        ins = None
                        for fcl in range(FQ):
                            ins = e.matmul(pm[:, 0:256], lhsT=Abuf[:, fcl, tb * 128:(tb + 1) * 128], rhs=wb[:, fcl * 256:(fcl + 1) * 256],
                                           start=(fcl == 0), stop=(fcl == FQ - 1))
                        return ins
                    S.op("pe", mmd, reads=[B_A, B_wb], writes=[B_pm])
                    S.op("dve", lambda e, pm=pm, tb=tb, nb=nb: e.tensor_tensor(
                        out=hv[:, tb, nb * 256:(nb + 1) * 256], in0=pm[:, 0:256], in1=hv[:, tb, nb * 256:(nb + 1) * 256], op=ALU.add),
                        reads=[B_pm, B_h], writes=[B_h])
        B_ss = nt_bufs[1]
        for tb in range(NTB):
            S.op("dve", lambda e, tb=tb: e.scalar_tensor_tensor(out=xn[:], in0=hv[:, tb, :], scalar=1.0, in1=hv[:, tb, :], op0=ALU.mult,
                                                                op1=ALU.mult, accum_out=ssq[:, 0:1]), reads=[B_h], writes=[nt_bufs[3], B_ss])
            S.op("act", lambda e: e.activation(out=ssq[:, 1:2], in_=ssq[:, 0:1], func=AF.Sqrt, bias=eps_t[:], scale=1.0 / D),
                 reads=[B_ss, B_const], writes=[B_ss])
            S.op("dve", lambda e: e.reciprocal(out=ssq[:, 2:3], in_=ssq[:, 1:2]), reads=[B_ss], writes=[B_ss])
            S.op("act", lambda e, tb=tb: e.activation(out=hv[:, tb, :], in_=hv[:, tb, :], func=AF.Copy, scale=ssq[:, 2:3]),
                 reads=[B_h, B_ss], writes=[B_h])
            S.op("dve", lambda e, tb=tb: e.tensor_tensor(out=hv[:, tb, :], in0=hv[:, tb, :], in1=nfB[:], op=ALU.mult),
                 reads=[B_h, B_const], writes=[B_h])
        S.dma("sp", q_y, lambda e, gtok0=gtok0: e.dma_start(
            out=y_out[gtok0:gtok0 + TT, :].rearrange("(b p) d -> p b d", p=128), in_=hv[:, :, :]), reads=[B_h], writes=[Buf("yout")])
    S.barrier()
    p3.close()
    top.close()
    S.emit()
    return nc


def _bucket_table():
    d = np.arange(256)
    nf = np.maximum(d, 1).astype(np.float32)
    large = 16 + (np.log(nf / np.float32(16)) / np.float32(np.log(128 / 16)) * np.float32(16)).astype(np.int32)
    large = np.minimum(large, 31)
    return np.where(d < 16, d, large)


def _prep(cfg, inp):
    import ml_dtypes
    D, DFF, NB, TT, KC, FC, T = cfg.D, cfg.DFF, cfg.NB, cfg.TT, cfg.KC, cfg.FC, cfg.T
    FQ, NFQ, NBO, NTILES = cfg.FQ, cfg.NFQ, cfg.NBO, cfg.NTILES
    f32 = np.float32
    x = np.asarray(inp["x"], f32)[0]
    hp = np.concatenate([np.zeros((LEAD_PAD, D), f32), np.asarray(inp["meta_tokens"], f32), x], axis=0)
    w_in = np.asarray(inp["w_in"], f32)[0]
    w_out = np.asarray(inp["w_out"], f32)[0]
    w_up = np.asarray(inp["w_up"], f32)[0]
    w_dn = np.asarray(inp["w_down"], f32)[0]
    w_in_l = np.ascontiguousarray(w_in.reshape(KC, 128, 40, 256).transpose(2, 1, 0, 3)).reshape(40 * 128, KC * 256)
    w_out_l = np.ascontiguousarray(w_out.reshape(32, 128, NBO, 256).transpose(2, 1, 0, 3)).reshape(NBO * 128, 32 * 256)
    w_up_l = np.ascontiguousarray(w_up.reshape(KC, 128, DFF // 256, 256).transpose(2, 1, 0, 3)).reshape(DFF // 256 * 128, KC * 256)
    w_dn_l = np.ascontiguousarray(w_dn.reshape(NFQ, FQ, 128, NBO, 256).transpose(0, 3, 2, 1, 4)).reshape(NFQ * NBO * 128, FQ * 256)

    def pc(v, n):
        return np.ascontiguousarray(np.asarray(v, f32).reshape(n, 128).T)
    g_mix = pc(inp["norm_mix"][0], KC)
    g_mlp = pc(inp["norm_mlp"][0], KC)
    nf_b = np.ascontiguousarray(np.broadcast_to(np.asarray(inp["norm_final"], f32)[None, :], (128, D)))
    cw = np.ascontiguousarray(np.asarray(inp["conv_w"], f32)[0].reshape(CONV_K, 16, 128).transpose(2, 1, 0)).reshape(128, 16 * CONV_K)
    merge = np.asarray(inp["merge_scale"], f32)[0]
    cvec = np.concatenate([pc(inp["conv_b"][0], 16), pc(inp["conv_ln_g"][0], 16), pc(inp["conv_ln_b"][0], 16),
                           pc(merge[ATTN_W:], 16)], axis=1)
    lamv = np.concatenate([np.broadcast_to(np.asarray(inp[k], f32)[0][None, :], (128, 128))
                           for k in ("lambda_q1", "lambda_k1", "lambda_q2", "lambda_k2")], axis=1)
    rel_bias = np.asarray(inp["rel_bias"], f32)
    bt = _bucket_table()
    kk = np.arange(128)[:, None]
    qq = np.arange(128)[None, :]
    d_prev = qq - kk + 128
    d_diag = np.maximum(qq - kk, 0)
    maskt = np.stack([np.ones((128, 128), f32), (qq >= kk).astype(f32)], axis=1).reshape(128, 256)
    eye = np.eye(128, dtype=f32).astype(ml_dtypes.bfloat16)
    subw = np.asarray(inp["subln_w"], f32)[0]
    maps = []
    for c in range(NCORES):
        r0 = c * NB * 128
        relb = np.stack([np.stack([rel_bias[bt[d_prev], 2 * c + m], rel_bias[bt[d_diag], 2 * c + m]], axis=1) for m in range(2)],
                        axis=1)
        b31 = np.broadcast_to(rel_bias[31, 2 * c:2 * c + 2][None, :], (128, 2))
        avec = np.concatenate([pc(subw, 2), pc(merge[c * 256:(c + 1) * 256], 2)], axis=1)
        p = np.arange(128)[:, None]
        iqk = np.stack([(r * 32 + qk * 16 + 2 * c + m) * 128 for r in range(NCORES) for qk in range(2) for m in range(2)])[None, :] + p
        iv = np.stack([(r * T + tb * 128) * 8 + c for r in range(NCORES) for tb in range(NB + 1)])[None, :] + p * 8
        ia = np.stack([(((hh * 8 + c) * NTILES + ti) * 2 + ec) * 128 for ti in range(NTILES) for hh in range(8) for ec in range(2)])[None, :] + p
        idxt = np.concatenate([iqk, iv, ia], axis=1).astype(np.int32)
        maps.append({
            "hp": np.ascontiguousarray(hp[r0:r0 + T]),
            "w_in_s": np.ascontiguousarray(w_in_l[c * 5 * 128:(c + 1) * 5 * 128]),
            "w_out_s": np.ascontiguousarray(w_out_l[c * (NBO // 8) * 128:(c + 1) * (NBO // 8) * 128]),
            "w_up_s": np.ascontiguousarray(w_up_l[c * (DFF // 2048) * 128:(c + 1) * (DFF // 2048) * 128]),
            "w_dn_s": np.ascontiguousarray(w_dn_l[c * (NFQ * NBO // 8) * 128:(c + 1) * (NFQ * NBO // 8) * 128]),
            "g_mix": g_mix, "g_mlp": g_mlp, "nf_b": nf_b, "cw": cw, "cvec": np.ascontiguousarray(cvec),
            "avec": np.ascontiguousarray(avec), "lamv": np.ascontiguousarray(lamv),
            "relb": np.ascontiguousarray(relb.reshape(128, 512).astype(f32)), "maskt": maskt,
            "b31": np.ascontiguousarray(b31.astype(f32)), "idxt": np.ascontiguousarray(idxt), "eye": eye,
        })
    return maps


def run_cfg(cfg, inp):
    nc = build(cfg)
    maps = _prep(cfg, inp)
    res = run_bass_kernel_spmd(nc, maps, core_ids=list(range(NCORES)))
    out = np.concatenate([np.asarray(res.results[c]["y"]) for c in range(NCORES)], axis=0)
    return out[None].astype(np.float32)


def kernel(**inputs):
    return run_cfg(Cfg(D=4096, DFF=16384, NB=16, TT=512), inputs)
```

```python
import contextlib
import numpy as np
import concourse.bass as bass
import concourse.mybir as mybir
from concourse.bass_utils import run_bass_kernel_spmd

F32 = mybir.dt.float32
BF16 = mybir.dt.bfloat16
I32 = mybir.dt.int32
CC_INC = 1
AF = mybir.ActivationFunctionType
ALU = mybir.AluOpType

NCORES = 8
N_META = 16
LEAD_PAD = 112
NMAPS = 16
NHEADS = 8
DK = 128
DV = 256
ATTN_W = 2048
CONV_W = 2048
MIX_W = 4096
IN_W = 10240
CONV_K = 31
EPS = 1e-6
LAMBDA_INIT = 0.8 - 0.6 * 1.0


class Cfg:
    def __init__(self, D=4096, DFF=16384, NB=16, TT=512):
        self.D, self.DFF, self.NB, self.TT = D, DFF, NB, TT
        self.KC = D // 128
        self.FC = DFF // 128
        self.T = (NB + 1) * 128
        self.NTILES = NB * 128 // TT
        self.FQ = min(32, self.FC)
        self.NFQ = self.FC // self.FQ
        self.NBO = D // 256
        self.SEQ = NCORES * NB * 128


class Buf:
    __slots__ = ("name", "w", "r")

    def __init__(self, name):
        self.name = name
        self.w = None
        self.r = []


class Sched:
    ENG = ("pe", "act", "dve", "pool", "sp")

    def __init__(self, nc):
        self.nc = nc
        self.sems = {}
        self.cnt = {}
        self.prog = {e: [] for e in self.ENG}
        self.seen = {e: {} for e in self.ENG}
        for e in ("pe", "act", "dve", "pool"):
            self._mksem(e)
        self.nstream = 0

    def _mksem(self, key):
        self.sems[key] = self.nc.alloc_semaphore(name="s_" + key)
        self.cnt[key] = 0

    def stream(self, name):
        key = "q_%s_%d" % (name, self.nstream)
        self.nstream += 1
        self._mksem(key)
        return key

    def _deps(self, reads, writes):
        need = {}

        def add(ev):
            if ev is None:
                return
            k, v = ev
            if need.get(k, 0) < v:
                need[k] = v
        for b in reads:
            add(b.w)
        for b in writes:
            add(b.w)
            for ev in b.r:
                add(ev)
        return need

    def _emit_waits(self, eng, need):
        seen = self.seen[eng]
        for k, v in need.items():
            if seen.get(k, 0) >= v:
                continue
            seen[k] = v
            self.prog[eng].append(("wait", k, v))

    def op(self, eng, fn, reads=(), writes=()):
        need = self._deps(reads, writes)
        self._emit_waits(eng, need)
        self.cnt[eng] += 1
        v = self.cnt[eng]
        self.prog[eng].append(("op", fn, eng, 1))
        ev = (eng, v)
        for b in writes:
            b.w = ev
            b.r = []
        for b in reads:
            if b not in writes:
                b.r.append(ev)
        return ev

    def dma(self, q, stream, fn, reads=(), writes=(), inc=16):
        need = self._deps(reads, writes)
        if self.cnt[stream] > 0:
            if need.get(stream, 0) < self.cnt[stream]:
                need[stream] = self.cnt[stream]
        self._emit_waits(q, need)
        self.cnt[stream] += inc
        v = self.cnt[stream]
        self.prog[q].append(("op", fn, stream, inc))
        ev = (stream, v)
        for b in writes:
            b.w = ev
            b.r = []
        for b in reads:
            if b not in writes:
                b.r.append(ev)
        return ev

    def barrier(self, exclude=()):
        need = {k: v for k, v in self.cnt.items() if v > 0 and k not in exclude}
        for e in self.ENG:
            self._emit_waits(e, dict(need))

    def emit(self):
        nc = self.nc
        handles = {"pe": "tensor", "act": "scalar", "dve": "vector", "pool": "gpsimd", "sp": "sync"}
        sems = self.sems

        def replay(e, eng):
            for item in self.prog[eng]:
                if item[0] == "wait":
                    e.wait_ge(sems[item[1]], item[2])
                else:
                    _, fn, key, inc = item
                    ins = fn(e)
                    ins.then_inc(sems[key], inc)
        with nc.Block() as block:
            @block.tensor
            def _(e):
                replay(e, "pe")

            @block.scalar
            def _(e):
                replay(e, "act")

            @block.vector
            def _(e):
                replay(e, "dve")

            @block.gpsimd
            def _(e):
                replay(e, "pool")

            @block.sync
            def _(e):
                replay(e, "sp")


class Ring:
    def __init__(self, items):
        self.items = items
        self.i = 0

    def next(self):
        it = self.items[self.i % len(self.items)]
        self.i += 1
        return it


def build(cfg):
    D, DFF, NB, TT, KC, FC, T = cfg.D, cfg.DFF, cfg.NB, cfg.TT, cfg.KC, cfg.FC, cfg.T
    FQ, NFQ, NBO, NTILES = cfg.FQ, cfg.NFQ, cfg.NBO, cfg.NTILES
    NTOK = NCORES * NB * 128
    nc = bass.Bass("TRN2", target_bir_lowering=False)
    S = Sched(nc)
    RG = [list(range(NCORES))]

    def dram_in(name, shape, dt=F32):
        return nc.dram_tensor(name, list(shape), dt, kind="ExternalInput").ap()

    def dram_tmp(name, shape, dt):
        return nc.dram_tensor(name, list(shape), dt).ap()

    hp = dram_in("hp", [T, D])
    w_in_s = dram_in("w_in_s", [5 * 128, KC * 256])
    w_out_s = dram_in("w_out_s", [NBO // 8 * 128, 32 * 256])
    w_up_s = dram_in("w_up_s", [DFF // 256 // 8 * 128, KC * 256])
    w_dn_s = dram_in("w_dn_s", [NFQ * NBO // 8 * 128, FQ * 256])
    g_mix_i = dram_in("g_mix", [128, KC])
    g_mlp_i = dram_in("g_mlp", [128, KC])
    nf_i = dram_in("nf_b", [128, D])
    cw_i = dram_in("cw", [128, 16 * CONV_K])
    cvec_i = dram_in("cvec", [128, 4 * 16])
    avec_i = dram_in("avec", [128, 4])
    lam_i = dram_in("lamv", [128, 4 * 128])
    relb_i = dram_in("relb", [128, 2 * 2 * 128])
    mask_i = dram_in("maskt", [128, 2 * 128])
    b31_i = dram_in("b31", [128, 2])
    NIDX = 32 + NCORES * (NB + 1) + NTILES * 16
    idx_i = dram_in("idxt", [128, NIDX], I32)
    y_out = nc.dram_tensor("y", [NB * 128, D], F32, kind="ExternalOutput").ap()

    w_in_b = dram_tmp("w_in_b", [5 * 128, KC * 256], BF16)
    w_in_g = dram_tmp("w_in_g", [40 * 128, KC * 256], BF16)
    w_out_b = dram_tmp("w_out_b", [NBO // 8 * 128, 32 * 256], BF16)
    w_out_g = dram_tmp("w_out_g", [NBO * 128, 32 * 256], BF16)
    w_up_b = dram_tmp("w_up_b", [DFF // 256 // 8 * 128, KC * 256], BF16)
    w_up_g = dram_tmp("w_up_g", [DFF // 256 * 128, KC * 256], BF16)
    w_dn_b = dram_tmp("w_dn_b", [NFQ * NBO // 8 * 128, FQ * 256], BF16)
    w_dn_g = dram_tmp("w_dn_g", [NFQ * NBO * 128, FQ * 256], BF16)
    qk_loc = dram_tmp("qk_loc", [32 * 128, T], BF16)
    qk_g = dram_tmp("qk_g", [NCORES * 32 * 128, T], BF16)
    v_loc = dram_tmp("v_loc", [T, ATTN_W], BF16)
    v_g = dram_tmp("v_g", [NCORES * T, ATTN_W], BF16)
    gt_loc = dram_tmp("gt_loc", [16 * 128, T], F32)
    at_loc = dram_tmp("at_loc", [NCORES * NTILES * 256, TT], BF16)
    at_g = dram_tmp("at_g", [NCORES * NCORES * NTILES * 256, TT], BF16)
    v_g8 = v_g.rearrange("t (h e) -> (t h) e", e=256)

    B_w_in_g, B_w_out_g, B_w_up_g, B_w_dn_g = Buf("w_in_g"), Buf("w_out_g"), Buf("w_up_g"), Buf("w_dn_g")
    B_qk_loc, B_qk_g, B_v_loc, B_v_g = Buf("qk_loc"), Buf("qk_g"), Buf("v_loc"), Buf("v_g")
    B_gt_loc, B_at_loc, B_at_g = Buf("gt_loc"), Buf("at_loc"), Buf("at_g")

    top = contextlib.ExitStack()

    def sb(stack, name, shape, dt):
        return stack.enter_context(nc.sbuf_tensor(name, list(shape), dt))

    def ps(stack, name, shape, dt):
        return stack.enter_context(nc.psum_tensor(name, list(shape), dt))

    ident = sb(top, "ident", [128, 128], BF16)
    ones_f = sb(top, "ones_f", [128, 128], F32)
    g_mix = sb(top, "g_mix_t", [128, KC], F32)
    g_mlp = sb(top, "g_mlp_t", [128, KC], F32)
    idx_t = sb(top, "idx_t", [128, NIDX], I32)
    cw = sb(top, "cw_t", [128, 16, CONV_K], F32)
    cvec = sb(top, "cvec_t", [128, 4, 16], F32)
    avec = sb(top, "avec_t", [128, 4], F32)
    wsc = sb(top, "wsc", [128, 2], F32)
    lamv = sb(top, "lamv_t", [128, 4, 128], F32)
    lamtmp = sb(top, "lamtmp", [128, 128], F32)
    lamd = sb(top, "lamd", [128, 4], F32)
    neg_lam = sb(top, "neg_lam", [128, 1], F32)
    relb = sb(top, "relb_t", [128, 2, 2, 128], F32)
    maskt = sb(top, "mask_t", [128, 2, 128], F32)
    b31 = sb(top, "b31_t", [128, 2], F32)
    Et = sb(top, "Et", [128, 2, 2, 128], F32)
    eps_t = sb(top, "eps_t", [128, 1], F32)
    B_const = Buf("const")
    q_const = S.stream("const")

    eye_i = dram_in("eye", [128, 128], BF16)

    def ld_const(dst, src):
        S.dma("sp", q_const, lambda e: e.dma_start(out=dst, in_=src), writes=[B_const])
    ld_const(ident[:], eye_i)
    ld_const(g_mix[:], g_mix_i)
    ld_const(g_mlp[:], g_mlp_i)
    ld_const(cw[:], cw_i.rearrange("p (c k) -> p c k", k=CONV_K))
    ld_const(cvec[:], cvec_i.rearrange("p (a c) -> p a c", c=16))
    ld_const(avec[:], avec_i)
    ld_const(lamv[:], lam_i.rearrange("p (a c) -> p a c", c=128))
    ld_const(relb[:], relb_i.rearrange("p (m s q) -> p m s q", m=2, s=2))
    ld_const(maskt[:], mask_i.rearrange("p (s q) -> p s q", s=2))
    ld_const(b31[:], b31_i)
    ld_const(idx_t[:], idx_i)

    def cop(eng, fn):
        S.op(eng, fn, reads=[B_const], writes=[B_const])
    cop("dve", lambda e: e.memset(ones_f[:], 1.0))
    cop("dve", lambda e: e.memset(eps_t[:], EPS))
    cop("dve", lambda e: e.scalar_tensor_tensor(out=wsc[:], in0=avec[:, 0:2], scalar=float(1.0 - LAMBDA_INIT),
                                                in1=avec[:, 2:4], op0=ALU.mult, op1=ALU.mult))
    for i in range(2):
        cop("dve", lambda e, i=i: e.scalar_tensor_tensor(out=lamtmp[:], in0=lamv[:, 2 * i, :], scalar=1.0,
                                                         in1=lamv[:, 2 * i + 1, :], op0=ALU.mult, op1=ALU.mult,
                                                         accum_out=lamd[:, i:i + 1]))
    cop("act", lambda e: e.activation(out=lamd[:, 2:4], in_=lamd[:, 0:2], func=AF.Exp))
    cop("dve", lambda e: e.tensor_tensor(out=neg_lam[:], in0=lamd[:, 3:4], in1=lamd[:, 2:3], op=ALU.subtract))
    cop("dve", lambda e: e.tensor_scalar(out=neg_lam[:], in0=neg_lam[:], scalar1=float(-LAMBDA_INIT), scalar2=None,
                                         op0=ALU.add))
    cop("dve", lambda e: e.tensor_scalar(out=b31[:], in0=b31[:], scalar1=-1.0, scalar2=None, op0=ALU.mult))
    for m in range(2):
        cop("act", lambda e, m=m: e.activation(out=Et[:, m], in_=relb[:, m], func=AF.Exp, bias=b31[:, m:m + 1], scale=1.0))
        cop("dve", lambda e, m=m: e.tensor_tensor(out=Et[:, m], in0=Et[:, m], in1=maskt[:], op=ALU.mult))

    q_wcast = S.stream("wcast")
    q_cc = S.stream("cc")
    for (src, bnc, gat, B_g) in ((w_in_s, w_in_b, w_in_g, B_w_in_g), (w_out_s, w_out_b, w_out_g, B_w_out_g),
                                 (w_up_s, w_up_b, w_up_g, B_w_up_g), (w_dn_s, w_dn_b, w_dn_g, B_w_dn_g)):
        Bb = Buf("bounce")
        rows = src.shape[0]
        step = 128
        for r0 in range(0, rows, step):
            S.dma("pool", q_wcast, lambda e, src=src, bnc=bnc, r0=r0: e.dma_start(out=bnc[r0:r0 + step, :], in_=src[r0:r0 + step, :]),
                  writes=[Bb])
        qos = {} if src is w_in_s else {"dma_qos": "P1"}
        S.dma("pool", q_cc, lambda e, bnc=bnc, gat=gat, qos=qos: e.collective_compute(
            "AllGather", ALU.bypass, replica_groups=RG, ins=[bnc], outs=[gat], **qos), reads=[Bb], writes=[B_g], inc=CC_INC)

    def norm_transpose(stk_bufs, src, B_src, gB, dstT, B_dst, col0):
        ssq, B_ss, xn, B_xn, pst_ring = stk_bufs
        S.op("dve", lambda e: e.scalar_tensor_tensor(out=xn[:], in0=src, scalar=1.0, in1=src, op0=ALU.mult, op1=ALU.mult,
                                                     accum_out=ssq[:, 0:1]), reads=[B_src], writes=[B_xn, B_ss])
        S.op("act", lambda e: e.activation(out=ssq[:, 1:2], in_=ssq[:, 0:1], func=AF.Sqrt, bias=eps_t[:], scale=1.0 / D),
             reads=[B_ss, B_const], writes=[B_ss])
        S.op("dve", lambda e: e.reciprocal(out=ssq[:, 2:3], in_=ssq[:, 1:2]), reads=[B_ss], writes=[B_ss])
        S.op("act", lambda e: e.activation(out=xn[:], in_=src, func=AF.Copy, scale=ssq[:, 2:3]), reads=[B_src, B_ss], writes=[B_xn])
        for k4 in range(0, KC, 4):
            pt, B_pt = pst_ring.next()

            def tr(e, k4=k4, pt=pt):
                ins = None
                for i in range(4):
                    ins = e.transpose(out=pt[:, i * 128:(i + 1) * 128], in_=xn[:, (k4 + i) * 128:(k4 + i + 1) * 128], identity=ident[:])
                return ins
            S.op("pe", tr, reads=[B_xn, B_const], writes=[B_pt])
            def ev(e, k4=k4, pt=pt):
                ins = None
                for i in range(4):
                    ins = e.tensor_scalar(out=dstT[:, k4 + i, col0:col0 + 128], in0=pt[:, i * 128:(i + 1) * 128],
                                          scalar1=gB[:, k4 + i:k4 + i + 1], scalar2=None, op0=ALU.mult)
                return ins
            S.op("dve", ev, reads=[B_pt, B_const], writes=[B_dst])

    p1 = contextlib.ExitStack()
    xnT = sb(p1, "xnT", [128, KC, T], BF16)
    B_xnT = Buf("xnT")
    xts = None
    p1a = contextlib.ExitStack()
    xn = sb(p1a, "xn", [128, D], BF16)
    xts = [(sb(p1a, "xt%d" % i, [128, D], F32), Buf("xt%d" % i), S.stream("xt")) for i in range(2)]
    ssq = sb(p1a, "ssq", [128, 4], F32)
    psts = Ring([(ps(p1a, "pst%d" % i, [128, 1024], BF16), Buf("pst%d" % i)) for i in range(2)])
    nt_bufs = (ssq, Buf("ssq"), xn, Buf("xn"), psts)
    for b in range(NB + 1):
        xt, B_xt, q_xt = xts[b % 2]
        S.dma("sp", q_xt, lambda e, xt=xt, b=b: e.dma_start(out=xt[:], in_=hp[b * 128:(b + 1) * 128, :]), writes=[B_xt])
        norm_transpose(nt_bufs, xt[:], B_xt, g_mix, xnT, B_xnT, b * 128)
    S.barrier(exclude=(q_wcast, q_cc))
    p1a.close()

    wbufs = Ring([(sb(p1, "wb%d" % i, [128, KC, 256], BF16), Buf("wb%d" % i), S.stream("wb")) for i in range(2)])
    pmm = Ring([(ps(p1, "pmm%d" % i, [128, 512], F32), Buf("pmm%d" % i)) for i in range(6)])
    qk_st = Ring([(sb(p1, "qkst%d" % i, [128, 512], BF16), Buf("qkst%d" % i), S.stream("qkst")) for i in range(3)])
    v_st = Ring([(sb(p1, "vst%d" % i, [128, 256], BF16), Buf("vst%d" % i), S.stream("vst")) for i in range(3)])
    g_st = Ring([(sb(p1, "gst%d" % i, [128, 512], F32), Buf("gst%d" % i), S.stream("gst")) for i in range(3)])
    sig = Ring([(sb(p1, "sig%d" % i, [128, 512], F32), Buf("sig%d" % i)) for i in range(2)])
    tchunks = [(t0, min(512, T - t0)) for t0 in range(0, T, 512)]

    def load_w(cb):
        wb, B_wb, q_wb = wbufs.next()
        S.dma("sp", q_wb, lambda e, wb=wb, cb=cb: e.dma_start(
            out=wb[:].rearrange("p k c -> p (k c)"), in_=w_in_g[cb * 128:(cb + 1) * 128, :]), reads=[B_w_in_g], writes=[B_wb])
        return wb, B_wb

    def fm_matmul(wb, B_wb, half, t0, n):
        pm, B_pm = pmm.next()

        def mm(e):
            ins = None
            for kc in range(KC):
                ins = e.matmul(pm[:, 0:n], lhsT=wb[:, kc, half * 128:(half + 1) * 128], rhs=xnT[:, kc, t0:t0 + n],
                               start=(kc == 0), stop=(kc == KC - 1))
            return ins
        S.op("pe", mm, reads=[B_wb, B_xnT], writes=[B_pm])
        return pm, B_pm

    evac_flip = [0]

    def evac_copy(dst, B_dst, src, B_s):
        evac_flip[0] ^= 1
        if evac_flip[0]:
            S.op("act", lambda e: e.activation(out=dst, in_=src, func=AF.Copy), reads=[B_s], writes=[B_dst])
        else:
            S.op("dve", lambda e: e.tensor_copy(out=dst, in_=src), reads=[B_s], writes=[B_dst])

    for cb in range(16):
        wb, B_wb = load_w(cb)
        for half in range(2):
            r0 = cb * 256 + half * 128
            for (t0, n) in tchunks:
                st, B_st, q_st = qk_st.next()
                pm, B_pm = fm_matmul(wb, B_wb, half, t0, n)
                evac_copy(st[:, 0:n], B_st, pm[:, 0:n], B_pm)
                S.dma("sp", q_st, lambda e, st=st, r0=r0, t0=t0, n=n: e.dma_start(out=qk_loc[r0:r0 + 128, t0:t0 + n], in_=st[:, 0:n]),
                      reads=[B_st], writes=[B_qk_loc])
    for hh in range(8):
        wb, B_wb = load_w(16 + hh)
        for tb in range(NB + 1):
            pm, B_pm = pmm.next()

            def mmv(e, pm=pm, wb=wb, tb=tb):
                ins = None
                for kc in range(KC):
                    ins = e.matmul(pm[:, 0:256], lhsT=xnT[:, kc, tb * 128:(tb + 1) * 128], rhs=wb[:, kc, :],
                                   start=(kc == 0), stop=(kc == KC - 1))
                return ins
            S.op("pe", mmv, reads=[B_wb, B_xnT], writes=[B_pm])
            st, B_st, q_st = v_st.next()
            evac_copy(st[:], B_st, pm[:, 0:256], B_pm)
            S.dma("sp", q_st, lambda e, st=st, tb=tb, hh=hh: e.dma_start(
                out=v_loc[tb * 128:(tb + 1) * 128, hh * 256:(hh + 1) * 256], in_=st[:]), reads=[B_st], writes=[B_v_loc])
    S.dma("pool", q_cc, lambda e: e.collective_compute("AllGather", ALU.bypass, replica_groups=RG, ins=[qk_loc], outs=[qk_g]),
          reads=[B_qk_loc], writes=[B_qk_g], inc=CC_INC)
    S.dma("pool", q_cc, lambda e: e.collective_compute("AllGather", ALU.bypass, replica_groups=RG, ins=[v_loc], outs=[v_g]),
          reads=[B_v_loc], writes=[B_v_g], inc=CC_INC)
    for j in range(8):
        wa, B_wa = load_w(24 + j)
        wg, B_wg = load_w(32 + j)
        for half in range(2):
            r0 = (j * 2 + half) * 128
            for (t0, n) in tchunks:
                st, B_st, q_st = g_st.next()
                pg, B_pg = fm_matmul(wg, B_wg, half, t0, n)
                pa, B_pa = fm_matmul(wa, B_wa, half, t0, n)
                sg, B_sg = sig.next()
                S.op("act", lambda e, sg=sg, pg=pg, n=n: e.activation(out=sg[:, 0:n], in_=pg[:, 0:n], func=AF.Sigmoid),
                     reads=[B_pg], writes=[B_sg])
                S.op("dve", lambda e, st=st, pa=pa, sg=sg, n=n: e.tensor_tensor(
                    out=st[:, 0:n], in0=pa[:, 0:n], in1=sg[:, 0:n], op=ALU.mult), reads=[B_pa, B_sg], writes=[B_st])
                S.dma("sp", q_st, lambda e, st=st, r0=r0, t0=t0, n=n: e.dma_start(out=gt_loc[r0:r0 + 128, t0:t0 + n], in_=st[:, 0:n]),
                      reads=[B_st], writes=[B_gt_loc])
    S.barrier()
    p1.close()

    p2 = contextlib.ExitStack()
    KT = sb(p2, "KT", [128, 2, NCORES, T], BF16)
    Vr = sb(p2, "Vr", [128, NCORES, NB + 1, 257], BF16)
    B_KT, B_Vr = Buf("KT"), Buf("Vr")
    q_kv = [S.stream("kv") for _ in range(4)]
    S.op("pool", lambda e: e.memset(Vr[:, :, :, 256:257], 1.0), writes=[B_Vr])
    def gather(q, out, src, col, reads, writes):
        S.dma("pool", q, lambda e: e.indirect_dma_start(out=out, out_offset=None, in_=src,
                                                        in_offset=bass.IndirectOffsetOnAxis(ap=idx_t[:, col:col + 1], axis=0)),
              reads=list(reads) + [B_const], writes=writes)
    for r in range(NCORES):
        for m in range(2):
            gather(q_kv[(2 * r + m) % 4], KT[:, m, r, :], qk_g[:, :], (r * 2 + 1) * 2 + m, [B_qk_g], [B_KT])
        for tb in range(NB + 1):
            gather(q_kv[tb % 4], Vr[:, r, tb, 0:256], v_g8[:, :], 32 + r * (NB + 1) + tb, [B_v_g], [B_Vr])
    S.op("pool", lambda e: e.memset(Vr[0:LEAD_PAD, 0, 0, :], 0.0), writes=[B_Vr])

    qrows = [(sb(p2, "qrow%d" % i, [128, 2, T], BF16), Buf("qrow%d" % i), S.stream("qrow")) for i in range(2)]
    cur_rq = [-1]
    pss = Ring([(ps(p2, "pss%d" % i, [128, 512], F32), Buf("pss%d" % i)) for i in range(3)])
    pso = [(ps(p2, "pso%d" % i, [128, 512], F32), Buf("pso%d" % i)) for i in range(4)]
    pstr = (ps(p2, "pstr", [128, 1024], BF16), Buf("pstr"))
    pts = Ring([(sb(p2, "pt%d" % i, [128, 512], BF16), Buf("pt%d" % i)) for i in range(4)])
    on = [(sb(p2, "on%d" % m, [128, 4, 256], F32), Buf("on%d" % m)) for m in range(2)]
    rs = sb(p2, "rs", [128, 8], F32)
    B_rs = Buf("rs")
    df = sb(p2, "df", [128, 4, 256], F32)
    B_df = Buf("df")
    dj = sb(p2, "dj", [128, 256], F32)
    B_dj = Buf("dj")
    sq = sb(p2, "sq", [128, 12], F32)
    B_sq = Buf("sq")
    dn = sb(p2, "dn", [128, 4, 256], BF16)
    B_dn = Buf("dn")
    ats = Ring([(sb(p2, "ats%d" % i, [128, 2, 512], BF16), Buf("ats%d" % i), S.stream("ats")) for i in range(2)])
    scale = float(DK ** -0.5)

    def blk_col(J):
        if J == 0:
            return 0, 0
        r, tb = (J - 1) // NB, (J - 1) % NB + 1
        return r, tb

    NG = NCORES * NB // 4
    LA = 2

    def front(gi, m, J):
        Jq0 = 1 + 4 * gi
        rq, tbq = blk_col(Jq0)
        qt, B_qt, q_qt = qrows[rq % 2]
        if cur_rq[0] != rq:
            cur_rq[0] = rq
            for mm_ in range(2):
                gather(q_qt, qt[:, mm_, :], qk_g[:, :], (rq * 2 + 0) * 2 + mm_, [B_qk_g], [B_qt])
        qoff = tbq * 128
        qlo = max(0, J - Jq0)
        c0 = qlo * 128
        rk, tbk = blk_col(J)
        psS, B_psS = pss.next()
        S.op("pe", lambda e: e.matmul(psS[:, c0:512], lhsT=KT[:, m, rk, tbk * 128:(tbk + 1) * 128],
                                      rhs=qt[:, m, qoff + c0:qoff + 512], start=True, stop=True),
             reads=[B_KT, B_qt], writes=[B_psS])
        pt, B_pt = pts.next()
        S.op("act", lambda e: e.activation(out=pt[:, c0:512], in_=psS[:, c0:512], func=AF.Exp, scale=scale),
             reads=[B_psS], writes=[B_pt])
        return (pt, B_pt, qlo, rk, tbk)

    def back(gi, m, J, st):
        pt, B_pt, qlo, rk, tbk = st
        Jq0 = 1 + 4 * gi
        for qb in range(qlo, 4):
            Jq = Jq0 + qb
            if J == Jq or J == Jq - 1:
                s_ = 1 if J == Jq else 0
                S.op("dve", lambda e, qb=qb, s_=s_: e.tensor_tensor(
                    out=pt[:, qb * 128:(qb + 1) * 128], in0=pt[:, qb * 128:(qb + 1) * 128], in1=Et[:, m, s_, :], op=ALU.mult),
                    reads=[B_pt, B_const], writes=[B_pt])

        def pv(e):
            ins = None
            for qb in range(qlo, 4):
                ins = e.matmul(pso[qb][0][:, 0:257], lhsT=pt[:, qb * 128:(qb + 1) * 128], rhs=Vr[:, rk, tbk, :],
                               start=(J == 0), stop=(J == Jq0 + qb))
            return ins
        S.op("pe", pv, reads=[B_pt, B_Vr], writes=[pso[qb][1] for qb in range(qlo, 4)])

    def unit_epilogue(gi, m):
        for qb in range(4):
            po, B_po = pso[qb]
            S.op("dve", lambda e, po=po, qb=qb: e.reciprocal(out=rs[:, m * 4 + qb:m * 4 + qb + 1], in_=po[:, 256:257]),
                 reads=[B_po], writes=[B_rs])
            S.op("act", lambda e, po=po, qb=qb: e.activation(out=on[m][0][:, qb, :], in_=po[:, 0:256], func=AF.Copy,
                                                             scale=rs[:, m * 4 + qb:m * 4 + qb + 1]),
                 reads=[B_po, B_rs], writes=[on[m][1]])

    def group_epilogue(gi):
        S.op("dve", lambda e: e.scalar_tensor_tensor(out=df[:], in0=on[1][0][:], scalar=neg_lam[:, 0:1], in1=on[0][0][:],
                                                     op0=ALU.mult, op1=ALU.add), reads=[on[0][1], on[1][1], B_const], writes=[B_df])
        for qb in range(4):
            S.op("dve", lambda e, qb=qb: e.scalar_tensor_tensor(out=dj[:], in0=df[:, qb, :], scalar=1.0, in1=df[:, qb, :], op0=ALU.mult,
                                                                op1=ALU.mult, accum_out=sq[:, qb:qb + 1]), reads=[B_df], writes=[B_dj, B_sq])
        S.op("act", lambda e: e.activation(out=sq[:, 4:8], in_=sq[:, 0:4], func=AF.Sqrt, bias=eps_t[:], scale=1.0 / DV),
             reads=[B_sq, B_const], writes=[B_sq])
        S.op("dve", lambda e: e.reciprocal(out=sq[:, 8:12], in_=sq[:, 4:8]), reads=[B_sq], writes=[B_sq])
        for qb in range(4):
            S.op("act", lambda e, qb=qb: e.activation(out=dn[:, qb, :], in_=df[:, qb, :], func=AF.Copy, scale=sq[:, 8 + qb:9 + qb]),
                 reads=[B_df, B_sq], writes=[B_dn])
        at, B_at, q_at = ats.next()
        ptr, B_ptr = pstr
        for ec in range(2):
            def tr2(e, ec=ec):
                ins = None
                for qb in range(4):
                    ins = e.transpose(out=ptr[:, qb * 128:(qb + 1) * 128], in_=dn[:, qb, ec * 128:(ec + 1) * 128], identity=ident[:])
                return ins
            S.op("pe", tr2, reads=[B_dn, B_const], writes=[B_ptr])
            S.op("dve", lambda e, ec=ec, at=at: e.tensor_scalar(out=at[:, ec, :], in0=ptr[:, 0:512], scalar1=wsc[:, ec:ec + 1],
                                                                scalar2=None, op0=ALU.mult), reads=[B_ptr, B_const], writes=[B_at])
        tsl, off = (gi * 512) // (NB * 128), (gi * 512) % (NB * 128)
        for sub in range(512 // TT):
            ti_ = off // TT + sub
            base = (tsl * NTILES + ti_) * 256
            S.dma("sp", q_at, lambda e, at=at, base=base, sub=sub: e.dma_start(
                out=at_loc[base:base + 256, :].rearrange("(c p) t -> p c t", p=128), in_=at[:, :, sub * TT:(sub + 1) * TT]),
                reads=[B_at], writes=[B_at_loc])

    steps = [(gi, m, J) for gi in range(NG) for m in range(2) for J in range(0, 1 + 4 * gi + 4)]
    states = {}
    for i in range(min(LA, len(steps))):
        states[i] = front(*steps[i])
    for i, (gi, m, J) in enumerate(steps):
        if i + LA < len(steps):
            states[i + LA] = front(*steps[i + LA])
        back(gi, m, J, states.pop(i))
        if J == 4 * gi + 4:
            unit_epilogue(gi, m)
            if m == 1:
                group_epilogue(gi)
    S.dma("pool", q_cc, lambda e: e.collective_compute("AllGather", ALU.bypass, replica_groups=RG, ins=[at_loc], outs=[at_g]),
          reads=[B_at_loc], writes=[B_at_g], inc=CC_INC)
    S.barrier()
    p2.close()

    p3 = contextlib.ExitStack()
    NTB = TT // 128
    HC = max(NTB * D, 24 * TT)
    hbuf = sb(p3, "hbuf", [128, HC], F32)
    B_h = Buf("h")
    NCH = max(KC, 32, FQ)
    Abuf = sb(p3, "Abuf", [128, NCH, TT], BF16)
    Bbuf = sb(p3, "Bbuf", [128, NCH, TT], BF16)
    B_A, B_B = Buf("A"), Buf("B")
    xn = sb(p3, "xn3", [128, D], BF16)
    nfB = sb(p3, "nfB", [128, D], F32)
    ld_const(nfB[:], nf_i)
    ssq = sb(p3, "ssq3", [128, 4], F32)
    psts = Ring([(ps(p3, "pst3_%d" % i, [128, 1024], BF16), Buf("pst3_%d" % i)) for i in range(2)])
    nt_bufs = (ssq, Buf("ssq3"), xn, Buf("xn3"), psts)
    pmm = Ring([(ps(p3, "pm3_%d" % i, [128, 512], F32), Buf("pm3_%d" % i)) for i in range(4)])
    pln = [(ps(p3, "pln%d" % i, [128, 512], F32), Buf("pln%d" % i)) for i in range(2)]
    wbufs = Ring([(sb(p3, "w3_%d" % i, [128, NCH * 256], BF16), Buf("w3_%d" % i), S.stream("w3")) for i in range(2)])
    gts = Ring([(sb(p3, "gt%d" % i, [128, TT + 32], F32), Buf("gt%d" % i), S.stream("gt")) for i in range(2)])
    rl = Ring([(sb(p3, "rl%d" % i, [128, TT], F32), Buf("rl%d" % i)) for i in range(2)])
    q_x = S.stream("x3")
    q_at3 = S.stream("at3")
    q_y = S.stream("yout")
    yall = hbuf[:, 0:16 * TT].rearrange("p (c t) -> p c t", t=TT)
    tmpv = hbuf[:, 16 * TT:24 * TT].rearrange("p (c t) -> p c t", t=TT)
    hv = hbuf[:, 0:NTB * D].rearrange("p (b d) -> p b d", d=D)

    for ti in range(NTILES):
        tl0 = 128 + ti * TT
        mm_sum = []
        for cc in range(16):
            gt, B_gt, q_gt = gts.next()
            S.dma("sp", q_gt, lambda e, gt=gt, cc=cc, tl0=tl0: e.dma_start(
                out=gt[:], in_=gt_loc[cc * 128:(cc + 1) * 128, tl0 - 32:tl0 + TT]), reads=[B_gt_loc], writes=[B_gt])

            def conv(e, gt=gt, cc=cc):
                ins = e.tensor_scalar(out=yall[:, cc, :], in0=gt[:, 2:2 + TT], scalar1=cw[:, cc, 0:1], scalar2=cvec[:, 0, cc:cc + 1],
                                      op0=ALU.mult, op1=ALU.add)
                return ins
            S.op("dve", conv, reads=[B_gt, B_const], writes=[B_h])
            for j in range(1, CONV_K):
                S.op("dve", lambda e, gt=gt, cc=cc, j=j: e.scalar_tensor_tensor(
                    out=yall[:, cc, :], in0=gt[:, 2 + j:2 + j + TT], scalar=cw[:, cc, j:j + 1], in1=yall[:, cc, :],
                    op0=ALU.mult, op1=ALU.add), reads=[B_gt, B_const, B_h], writes=[B_h])
        pmu, B_pmu = pln[0]
        pq, B_pq = pln[1]
        B_tmp = B_h
        for cc in range(16):
            S.op("pe", lambda e, cc=cc: e.matmul(pmu[:, 0:TT], lhsT=ones_f[:], rhs=yall[:, cc, :], start=(cc == 0), stop=(cc == 15)),
                 reads=[B_h, B_const], writes=[B_pmu])
        for cc in range(16):
            S.op("act", lambda e, cc=cc: e.activation(out=tmpv[:, cc % 2, :], in_=yall[:, cc, :], func=AF.Square), reads=[B_h], writes=[B_h])
            S.op("pe", lambda e, cc=cc: e.matmul(pq[:, 0:TT], lhsT=ones_f[:], rhs=tmpv[:, cc % 2, :], start=(cc == 0), stop=(cc == 15)),
                 reads=[B_h, B_const], writes=[B_pq])
        inv = 1.0 / CONV_W
        S.op("act", lambda e: e.activation(out=tmpv[:, 2, :], in_=pmu[:, 0:TT], func=AF.Copy, scale=inv), reads=[B_pmu], writes=[B_h])
        S.op("dve", lambda e: e.tensor_tensor(out=tmpv[:, 3, :], in0=tmpv[:, 2, :], in1=tmpv[:, 2, :], op=ALU.mult), reads=[B_h], writes=[B_h])
        S.op("dve", lambda e: e.scalar_tensor_tensor(out=tmpv[:, 3, :], in0=pq[:, 0:TT], scalar=inv, in1=tmpv[:, 3, :], op0=ALU.mult,
                                                     op1=ALU.subtract), reads=[B_pq, B_h], writes=[B_h])
        S.op("act", lambda e: e.activation(out=tmpv[:, 3, :], in_=tmpv[:, 3, :], func=AF.Sqrt, bias=eps_t[:], scale=1.0),
             reads=[B_h, B_const], writes=[B_h])
        S.op("dve", lambda e: e.reciprocal(out=tmpv[:, 4, :], in_=tmpv[:, 3, :]), reads=[B_h], writes=[B_h])
        S.op("dve", lambda e: e.scalar_tensor_tensor(out=tmpv[:, 5, :], in0=tmpv[:, 2, :], scalar=-1.0, in1=tmpv[:, 4, :], op0=ALU.mult,
                                                     op1=ALU.mult), reads=[B_h], writes=[B_h])
        for cc in range(16):
            z = tmpv[:, 6 + cc % 2, :]
            S.op("dve", lambda e, cc=cc, z=z: e.tensor_tensor(out=z, in0=yall[:, cc, :], in1=tmpv[:, 4, :], op=ALU.mult), reads=[B_h], writes=[B_h])
            S.op("pool", lambda e, z=z: e.tensor_tensor(out=z, in0=z, in1=tmpv[:, 5, :], op=ALU.add), reads=[B_h], writes=[B_h])
            S.op("act", lambda e, cc=cc, z=z: e.activation(out=z, in_=z, func=AF.Silu, scale=cvec[:, 1, cc:cc + 1], bias=cvec[:, 2, cc:cc + 1]),
                 reads=[B_h, B_const], writes=[B_h])
            S.op("pool", lambda e, cc=cc, z=z: e.tensor_scalar(out=Abuf[:, 16 + cc, :], in0=z, scalar1=cvec[:, 3, cc:cc + 1], scalar2=None,
                                                               op0=ALU.mult), reads=[B_h, B_const], writes=[B_A])
        gtok0 = ti * TT
        for hc in range(16):
            gather(q_at3, Abuf[:, hc, :], at_g[:, :], 32 + NCORES * (NB + 1) + ti * 16 + hc, [B_at_g], [B_A])
        S.dma("sp", q_x, lambda e, tl0=tl0: e.dma_start(out=hv[:, :, :], in_=hp[tl0:tl0 + TT, :].rearrange("(b p) d -> p b d", p=128)),
              writes=[B_h])
        for nb in range(NBO):
            wb, B_wb, q_wb = wbufs.next()
            S.dma("sp", q_wb, lambda e, wb=wb, nb=nb: e.dma_start(out=wb[:, 0:32 * 256], in_=w_out_g[nb * 128:(nb + 1) * 128, :]),
                  reads=[B_w_out_g], writes=[B_wb])
            for tb in range(NTB):
                pm, B_pm = pmm.next()

                def mmo(e, pm=pm, wb=wb, tb=tb):
                    ins = None
                    for ck in range(32):
                        ins = e.matmul(pm[:, 0:256], lhsT=Abuf[:, ck, tb * 128:(tb + 1) * 128], rhs=wb[:, ck * 256:(ck + 1) * 256],
                                       start=(ck == 0), stop=(ck == 31))
                    return ins
                S.op("pe", mmo, reads=[B_A, B_wb], writes=[B_pm])
                S.op("dve", lambda e, pm=pm, tb=tb, nb=nb: e.tensor_tensor(
                    out=hv[:, tb, nb * 256:(nb + 1) * 256], in0=pm[:, 0:256], in1=hv[:, tb, nb * 256:(nb + 1) * 256], op=ALU.add),
                    reads=[B_pm, B_h], writes=[B_h])
        for tb in range(NTB):
            norm_transpose(nt_bufs, hv[:, tb, :], B_h, g_mlp, Bbuf, B_B, tb * 128)
        for fq in range(NFQ):
            for fb in range(FQ // 2):
                wb, B_wb, q_wb = wbufs.next()
                fbg = fq * (FQ // 2) + fb
                S.dma("sp", q_wb, lambda e, wb=wb, fbg=fbg: e.dma_start(out=wb[:, 0:KC * 256], in_=w_up_g[fbg * 128:(fbg + 1) * 128, :]),
                      reads=[B_w_up_g], writes=[B_wb])
                for half in range(2):
                    pm, B_pm = pmm.next()

                    def mmu(e, pm=pm, wb=wb, half=half):
                        ins = None
                        for kc in range(KC):
                            ins = e.matmul(pm[:, 0:TT], lhsT=wb[:, kc * 256 + half * 128:kc * 256 + (half + 1) * 128], rhs=Bbuf[:, kc, :],
                                           start=(kc == 0), stop=(kc == KC - 1))
                        return ins
                    S.op("pe", mmu, reads=[B_B, B_wb], writes=[B_pm])
                    r_, B_r = rl.next()
                    S.op("act", lambda e, r_=r_, pm=pm: e.activation(out=r_[:], in_=pm[:, 0:TT], func=AF.Relu), reads=[B_pm], writes=[B_r])
                    fcl = fb * 2 + half
                    S.op("pool" if (fcl % 2) else "dve", lambda e, r_=r_, fcl=fcl: e.tensor_tensor(out=Abuf[:, fcl, :], in0=r_[:], in1=r_[:], op=ALU.mult),
                         reads=[B_r], writes=[B_A])
            for nb in range(NBO):
                wb, B_wb, q_wb = wbufs.next()
                S.dma("sp", q_wb, lambda e, wb=wb, fq=fq, nb=nb: e.dma_start(
                    out=wb[:, 0:FQ * 256], in_=w_dn_g[(fq * NBO + nb) * 128:(fq * NBO + nb + 1) * 128, :]), reads=[B_w_dn_g], writes=[B_wb])
                for tb in range(NTB):
                    pm, B_pm = pmm.next()

                    def mmd(e, pm=pm, wb=wb, tb=tb):
                        ins = None
                        for fcl in range(FQ):
                            ins = e.matmul(pm[:, 0:256], lhsT=Abuf[:, fcl, tb * 128:(tb + 1) * 128], rhs=wb[:, fcl * 256:(fcl + 1) * 256],
                                           start=(fcl == 0), stop=(fcl == FQ - 1))
                        return ins
                    S.op("pe", mmd, reads=[B_A, B_wb], writes=[B_pm])
                    S.op("dve", lambda e, pm=pm, tb=tb, nb=nb: e.tensor_tensor(
                        out=hv[:, tb, nb * 256:(nb + 1) * 256], in0=pm[:, 0:256], in1=hv[:, tb, nb * 256:(nb + 1) * 256], op=ALU.add),
                        reads=[B_pm, B_h], writes=[B_h])
        B_ss = nt_bufs[1]
        for tb in range(NTB):
            S.op("dve", lambda e, tb=tb: e.scalar_tensor_tensor(out=xn[:], in0=hv[:, tb, :], scalar=1.0, in1=hv[:, tb, :], op0=ALU.mult,
                                                                op1=ALU.mult, accum_out=ssq[:, 0:1]), reads=[B_h], writes=[nt_bufs[3], B_ss])
            S.op("act", lambda e: e.activation(out=ssq[:, 1:2], in_=ssq[:, 0:1], func=AF.Sqrt, bias=eps_t[:], scale=1.0 / D),
                 reads=[B_ss, B_const], writes=[B_ss])
            S.op("dve", lambda e: e.reciprocal(out=ssq[:, 2:3], in_=ssq[:, 1:2]), reads=[B_ss], writes=[B_ss])
            S.op("act", lambda e, tb=tb: e.activation(out=hv[:, tb, :], in_=hv[:, tb, :], func=AF.Copy, scale=ssq[:, 2:3]),
                 reads=[B_h, B_ss], writes=[B_h])
            S.op("dve", lambda e, tb=tb: e.tensor_tensor(out=hv[:, tb, :], in0=hv[:, tb, :], in1=nfB[:], op=ALU.mult),
                 reads=[B_h, B_const], writes=[B_h])
        S.dma("sp", q_y, lambda e, gtok0=gtok0: e.dma_start(
            out=y_out[gtok0:gtok0 + TT, :].rearrange("(b p) d -> p b d", p=128), in_=hv[:, :, :]), reads=[B_h], writes=[Buf("yout")])
    S.barrier()
    p3.close()
    top.close()
    S.emit()
    return nc


def _bucket_table():
    d = np.arange(256)
    nf = np.maximum(d, 1).astype(np.float32)
    large = 16 + (np.log(nf / np.float32(16)) / np.float32(np.log(128 / 16)) * np.float32(16)).astype(np.int32)
    large = np.minimum(large, 31)
    return np.where(d < 16, d, large)


def _prep(cfg, inp):
    import ml_dtypes
    D, DFF, NB, TT, KC, FC, T = cfg.D, cfg.DFF, cfg.NB, cfg.TT, cfg.KC, cfg.FC, cfg.T
    FQ, NFQ, NBO, NTILES = cfg.FQ, cfg.NFQ, cfg.NBO, cfg.NTILES
    f32 = np.float32
    x = np.asarray(inp["x"], f32)[0]
    hp = np.concatenate([np.zeros((LEAD_PAD, D), f32), np.asarray(inp["meta_tokens"], f32), x], axis=0)
    w_in = np.asarray(inp["w_in"], f32)[0]
    w_out = np.asarray(inp["w_out"], f32)[0]
    w_up = np.asarray(inp["w_up"], f32)[0]
    w_dn = np.asarray(inp["w_down"], f32)[0]
    w_in_l = np.ascontiguousarray(w_in.reshape(KC, 128, 40, 256).transpose(2, 1, 0, 3)).reshape(40 * 128, KC * 256)
    w_out_l = np.ascontiguousarray(w_out.reshape(32, 128, NBO, 256).transpose(2, 1, 0, 3)).reshape(NBO * 128, 32 * 256)
    w_up_l = np.ascontiguousarray(w_up.reshape(KC, 128, DFF // 256, 256).transpose(2, 1, 0, 3)).reshape(DFF // 256 * 128, KC * 256)
    w_dn_l = np.ascontiguousarray(w_dn.reshape(NFQ, FQ, 128, NBO, 256).transpose(0, 3, 2, 1, 4)).reshape(NFQ * NBO * 128, FQ * 256)

    def pc(v, n):
        return np.ascontiguousarray(np.asarray(v, f32).reshape(n, 128).T)
    g_mix = pc(inp["norm_mix"][0], KC)
    g_mlp = pc(inp["norm_mlp"][0], KC)
    nf_b = np.ascontiguousarray(np.broadcast_to(np.asarray(inp["norm_final"], f32)[None, :], (128, D)))
    cw = np.ascontiguousarray(np.asarray(inp["conv_w"], f32)[0].reshape(CONV_K, 16, 128).transpose(2, 1, 0)).reshape(128, 16 * CONV_K)
    merge = np.asarray(inp["merge_scale"], f32)[0]
    cvec = np.concatenate([pc(inp["conv_b"][0], 16), pc(inp["conv_ln_g"][0], 16), pc(inp["conv_ln_b"][0], 16),
                           pc(merge[ATTN_W:], 16)], axis=1)
    lamv = np.concatenate([np.broadcast_to(np.asarray(inp[k], f32)[0][None, :], (128, 128))
                           for k in ("lambda_q1", "lambda_k1", "lambda_q2", "lambda_k2")], axis=1)
    rel_bias = np.asarray(inp["rel_bias"], f32)
    bt = _bucket_table()
    kk = np.arange(128)[:, None]
    qq = np.arange(128)[None, :]
    d_prev = qq - kk + 128
    d_diag = np.maximum(qq - kk, 0)
    maskt = np.stack([np.ones((128, 128), f32), (qq >= kk).astype(f32)], axis=1).reshape(128, 256)
    eye = np.eye(128, dtype=f32).astype(ml_dtypes.bfloat16)
    subw = np.asarray(inp["subln_w"], f32)[0]
    maps = []
    for c in range(NCORES):
        r0 = c * NB * 128
        relb = np.stack([np.stack([rel_bias[bt[d_prev], 2 * c + m], rel_bias[bt[d_diag], 2 * c + m]], axis=1) for m in range(2)],
                        axis=1)
        b31 = np.broadcast_to(rel_bias[31, 2 * c:2 * c + 2][None, :], (128, 2))
        avec = np.concatenate([pc(subw, 2), pc(merge[c * 256:(c + 1) * 256], 2)], axis=1)
        p = np.arange(128)[:, None]
        iqk = np.stack([(r * 32 + qk * 16 + 2 * c + m) * 128 for r in range(NCORES) for qk in range(2) for m in range(2)])[None, :] + p
        iv = np.stack([(r * T + tb * 128) * 8 + c for r in range(NCORES) for tb in range(NB + 1)])[None, :] + p * 8
        ia = np.stack([(((hh * 8 + c) * NTILES + ti) * 2 + ec) * 128 for ti in range(NTILES) for hh in range(8) for ec in range(2)])[None, :] + p
        idxt = np.concatenate([iqk, iv, ia], axis=1).astype(np.int32)
        maps.append({
            "hp": np.ascontiguousarray(hp[r0:r0 + T]),
            "w_in_s": np.ascontiguousarray(w_in_l[c * 5 * 128:(c + 1) * 5 * 128]),
            "w_out_s": np.ascontiguousarray(w_out_l[c * (NBO // 8) * 128:(c + 1) * (NBO // 8) * 128]),
            "w_up_s": np.ascontiguousarray(w_up_l[c * (DFF // 2048) * 128:(c + 1) * (DFF // 2048) * 128]),
            "w_dn_s": np.ascontiguousarray(w_dn_l[c * (NFQ * NBO // 8) * 128:(c + 1) * (NFQ * NBO // 8) * 128]),
            "g_mix": g_mix, "g_mlp": g_mlp, "nf_b": nf_b, "cw": cw, "cvec": np.ascontiguousarray(cvec),
            "avec": np.ascontiguousarray(avec), "lamv": np.ascontiguousarray(lamv),
            "relb": np.ascontiguousarray(relb.reshape(128, 512).astype(f32)), "maskt": maskt,
            "b31": np.ascontiguousarray(b31.astype(f32)), "idxt": np.ascontiguousarray(idxt), "eye": eye,
        })
    return maps


def run_cfg(cfg, inp):
    nc = build(cfg)
    maps = _prep(cfg, inp)
    res = run_bass_kernel_spmd(nc, maps, core_ids=list(range(NCORES)))
    out = np.concatenate([np.asarray(res.results[c]["y"]) for c in range(NCORES)], axis=0)
    return out[None].astype(np.float32)


def kernel(**inputs):
    return run_cfg(Cfg(D=4096, DFF=16384, NB=16, TT=512), inputs)
```
